# Optimizing a Trainium2 kernel written in Bass

```python
import jax, jax.numpy as jnp
from jax import lax
import numpy as np

D_MODEL = 1024
BATCH = 2
SEQ = 8192
DEPTH = 4

N_MIXERS = 3
N_META = 16
D_FF = 2816
SC_WIDTH = 3
CONF_WIDTH = 31
N_HEADS = 16
HEAD_DIM = D_MODEL // N_HEADS
BLOCK = 128
EPS = 1e-6
MASK_VALUE = -1e30
N_A = (DEPTH + 2) // 3
N_B = (DEPTH + 1) // 3
N_C = DEPTH // 3

kernel_name = "hybrid_shortconv_conformer_fox_macaron"


def rms_norm(x, g):
    xf = x.astype(jnp.float32)
    y = xf * lax.rsqrt(jnp.mean(xf * xf, axis=-1, keepdims=True) + EPS)
    return (y * g.astype(jnp.float32)).astype(x.dtype)


def layer_norm(x, g, b):
    xf = x.astype(jnp.float32)
    mu = jnp.mean(xf, axis=-1, keepdims=True)
    xc = xf - mu
    var = jnp.mean(xc * xc, axis=-1, keepdims=True)
    y = xc * lax.rsqrt(var + EPS) * g.astype(jnp.float32) + b.astype(jnp.float32)
    return y.astype(x.dtype)


def causal_depthwise_conv(x, w):
    k = w.shape[0]
    return lax.conv_general_dilated(
        x, w[:, None, :].astype(x.dtype), window_strides=(1,), padding=[(k - 1, 0)],
        dimension_numbers=("NWC", "WIO", "NWC"), feature_group_count=x.shape[-1])


def swiglu(h, w_gate, w_up, w_down):
    return (jax.nn.silu(h @ w_gate) * (h @ w_up)) @ w_down


def short_conv_mixer(h, w_in, conv_w, w_out):
    b_gate, c_gate, v = jnp.split(h @ w_in, 3, axis=-1)
    y = b_gate * causal_depthwise_conv(c_gate * v, conv_w)
    return y @ w_out


def conformer_conv_mixer(h, w_in, conv_w, conv_b, ln_g, ln_b, w_out):
    a, g = jnp.split(h @ w_in, 2, axis=-1)
    u = a * jax.nn.sigmoid(g)
    u = causal_depthwise_conv(u, conv_w) + conv_b
    u = jax.nn.silu(layer_norm(u, ln_g, ln_b))
    return u @ w_out


def forgetting_attention(h, w_in, b_f, q_g, k_g, w_out):
    bsz, L, _ = h.shape
    proj = h @ w_in
    q = proj[..., :D_MODEL].reshape(bsz, L, N_HEADS, HEAD_DIM)
    k = proj[..., D_MODEL:2 * D_MODEL].reshape(bsz, L, N_HEADS, HEAD_DIM)
    v = proj[..., 2 * D_MODEL:3 * D_MODEL].reshape(bsz, L, N_HEADS, HEAD_DIM)
    f_logit = proj[..., 3 * D_MODEL:] + b_f
    q = rms_norm(q, q_g)
    k = rms_norm(k, k_g)
    log_f = jax.nn.log_sigmoid(f_logit.astype(jnp.float32))
    cum = jnp.cumsum(log_f, axis=1)
    pad = (-L) % BLOCK
    lp = L + pad
    n_blocks = lp // BLOCK
    pad4 = ((0, 0), (pad, 0), (0, 0), (0, 0))
    q = jnp.pad(q, pad4).transpose(0, 2, 1, 3)
    k = jnp.pad(k, pad4).transpose(0, 2, 1, 3)
    v = jnp.pad(v, pad4).transpose(0, 2, 1, 3)
    cum = jnp.pad(cum, ((0, 0), (pad, 0), (0, 0))).transpose(0, 2, 1)
    kpos = jnp.arange(lp)
    scale = HEAD_DIM ** -0.5

    def one_block(i):
        start = i * BLOCK
        qb = lax.dynamic_slice_in_dim(q, start, BLOCK, axis=2)
        cq = lax.dynamic_slice_in_dim(cum, start, BLOCK, axis=2)
        s = jnp.einsum("bhqd,bhkd->bhqk", qb, k, preferred_element_type=jnp.float32) * scale
        s = s + cq[..., :, None] - cum[..., None, :]
        qpos = start + jnp.arange(BLOCK)
        mask = (kpos[None, :] <= qpos[:, None]) & (kpos[None, :] >= pad)
        s = jnp.where(mask, s, MASK_VALUE)
        p = jax.nn.softmax(s, axis=-1).astype(v.dtype)
        return jnp.einsum("bhqk,bhkd->bhqd", p, v)

    o = lax.map(one_block, jnp.arange(n_blocks))
    o = o.transpose(1, 0, 3, 2, 4).reshape(bsz, lp, D_MODEL)[:, pad:]
    return o @ w_out


def setup_inputs(seed: int = 0) -> dict:
    key = jax.random.key(seed)
    ks = jax.random.split(key, 24)
    D, F, H = D_MODEL, D_FF, N_HEADS
    nrm = lambda k, shape, fan: jax.random.normal(k, shape, jnp.float32) * (fan ** -0.5)
    gain = lambda k, shape: 1.0 + 0.05 * jax.random.normal(k, shape, jnp.float32)
    small = lambda k, shape: 0.02 * jax.random.normal(k, shape, jnp.float32)
    return {
        "x": jax.random.normal(ks[0], (BATCH, SEQ, D), jnp.float32),
        "meta": jax.random.normal(ks[1], (N_META, D), jnp.float32),
        "ffn_norm": gain(ks[2], (DEPTH, 2, D)),
        "ffn_w_gate": nrm(ks[3], (DEPTH, 2, D, F), D),
        "ffn_w_up": nrm(ks[4], (DEPTH, 2, D, F), D),
        "ffn_w_down": nrm(ks[5], (DEPTH, 2, F, D), F),
        "mix_norm": gain(ks[6], (DEPTH, D)),
        "a_w_in": nrm(ks[7], (N_A, D, 3 * D), D),
        "a_conv": nrm(ks[8], (N_A, SC_WIDTH, D), SC_WIDTH),
        "a_w_out": nrm(ks[9], (N_A, D, D), D),
        "b_w_in": nrm(ks[10], (N_B, D, 2 * D), D),
        "b_conv": nrm(ks[11], (N_B, CONF_WIDTH, D), CONF_WIDTH),
        "b_conv_bias": small(ks[12], (N_B, D)),
        "b_ln_g": gain(ks[13], (N_B, D)),
        "b_ln_b": small(ks[14], (N_B, D)),
        "b_w_out": nrm(ks[15], (N_B, D, D), D),
        "c_w_in": nrm(ks[16], (N_C, D, 3 * D + H), D),
        "c_b_f": jax.random.uniform(ks[17], (N_C, H), jnp.float32, 1.0, 4.0),
        "c_q_norm": gain(ks[18], (N_C, HEAD_DIM)),
        "c_k_norm": gain(ks[19], (N_C, HEAD_DIM)),
        "c_w_out": nrm(ks[20], (N_C, D, D), D),
    }


def reference(x, meta, ffn_norm, ffn_w_gate, ffn_w_up, ffn_w_down, mix_norm,
              a_w_in, a_conv, a_w_out,
              b_w_in, b_conv, b_conv_bias, b_ln_g, b_ln_b, b_w_out,
              c_w_in, c_b_f, c_q_norm, c_k_norm, c_w_out):
    bsz = x.shape[0]
    meta_b = jnp.broadcast_to(meta[None].astype(x.dtype), (bsz, N_META, D_MODEL))
    h = jnp.concatenate([meta_b, x], axis=1)
    for i in range(DEPTH):
        m, j = i % N_MIXERS, i // N_MIXERS
        h = h + 0.5 * swiglu(rms_norm(h, ffn_norm[i, 0]), ffn_w_gate[i, 0], ffn_w_up[i, 0], ffn_w_down[i, 0])
        u = rms_norm(h, mix_norm[i])
        if m == 0:
            mix = short_conv_mixer(u, a_w_in[j], a_conv[j], a_w_out[j])
        elif m == 1:
            mix = conformer_conv_mixer(u, b_w_in[j], b_conv[j], b_conv_bias[j], b_ln_g[j], b_ln_b[j], b_w_out[j])
        else:
            mix = forgetting_attention(u, c_w_in[j], c_b_f[j], c_q_norm[j], c_k_norm[j], c_w_out[j])
        h = h + mix
        h = h + 0.5 * swiglu(rms_norm(h, ffn_norm[i, 1]), ffn_w_gate[i, 1], ffn_w_up[i, 1], ffn_w_down[i, 1])
    return h[:, N_META:]
```

```python
import numpy as np
from contextlib import ExitStack
import concourse.bass as bass
import concourse.mybir as mybir
from concourse.bass_utils import run_bass_kernel_spmd

F32 = mybir.dt.float32
BF16 = mybir.dt.bfloat16
AF = mybir.ActivationFunctionType
ALU = mybir.AluOpType

D = 1024
DFF = 2816
NKC = 8
SEQ = 8192
NMETA = 16
NH = 16
HD = 64
EPS = 1e-6
NCORE = 8
CH = 2048
THIRDS = [(0, 8), (8, 8), (16, 6)]
NEG = -30000.0


class Buf:
    __slots__ = ("name", "w", "r")

    def __init__(self, name):
        self.name = name
        self.w = None
        self.r = {}


class Op:
    __slots__ = ("fn", "waits", "signal", "dma")

    def __init__(self, fn, dma):
        self.fn = fn
        self.waits = []
        self.signal = False
        self.dma = dma


ENGS = ["pe", "act", "dve", "pool", "sp"]


class Prog:
    def __init__(self):
        self.ops = {e: [] for e in ENGS}
        self.waited = {e: {} for e in ENGS}
        self.dma_cnt = []
        self.strict = False

    def dsem(self):
        self.dma_cnt.append(0)
        return len(self.dma_cnt) - 1

    def _need(self, eng, op, key, val):
        if key[0] == "eng" and key[1] == eng and not (self.strict and eng != "pe"):
            return
        if self.waited[eng].get(key, 0) >= val:
            return
        self.waited[eng][key] = val
        op.waits.append((key, val))
        if key[0] == "eng":
            self.ops[key[1]][val - 1].signal = True

    def add(self, eng, fn, reads=(), writes=(), dma=None):
        op = Op(fn, dma)
        for b in reads:
            if b.w is not None:
                self._need(eng, op, *b.w)
        for b in writes:
            if b.w is not None:
                self._need(eng, op, *b.w)
            for k, v in b.r.items():
                self._need(eng, op, k, v)
        self.ops[eng].append(op)
        if dma is not None:
            self.dma_cnt[dma] += 16
            ev = (("dma", dma), self.dma_cnt[dma])
        else:
            ev = (("eng", eng), len(self.ops[eng]))
        for b in reads:
            if b.r.get(ev[0], 0) < ev[1]:
                b.r[ev[0]] = ev[1]
        for b in writes:
            b.w = ev
            b.r = {}
        return op

    def emit(self, nc, es):
        esem = {e: es.enter_context(nc.semaphore("se_" + e)) for e in ENGS}
        dsem = [es.enter_context(nc.semaphore("sd_%d" % i)) for i in range(len(self.dma_cnt))]
        pref = {}
        for e in ENGS:
            c = 0
            arr = []
            for op in self.ops[e]:
                if op.signal and op.dma is None:
                    c += 1
                arr.append(c)
            pref[e] = arr
        block = es.enter_context(nc.Block())

        def run(name, eng):
            for op in self.ops[name]:
                for key, val in op.waits:
                    if key[0] == "eng":
                        eng.wait_ge(esem[key[1]], pref[key[1]][val - 1])
                    else:
                        eng.wait_ge(dsem[key[1]], val)
                inst = op.fn(eng)
                if op.dma is not None:
                    inst.then_inc(dsem[op.dma], 16)
                elif op.signal:
                    inst.then_inc(esem[name], 1)

        @block.tensor
        def _(e):
            run("pe", e)

        @block.scalar
        def _(e):
            run("act", e)

        @block.vector
        def _(e):
            run("dve", e)

        @block.gpsimd
        def _(e):
            run("pool", e)

        @block.sync
        def _(e):
            run("sp", e)


class KB:
    def __init__(self, pre):
        self.nc = bass.Bass("TRN2", target_bir_lowering=False)
        self.P = Prog()
        self.es = ExitStack()
        self.pre = pre
        self.NT = pre + CH
        self.tiles = [(0, pre)] + [(pre + 512 * i, 512) for i in range(4)]
        nc, es = self.nc, self.es
        NT = self.NT
        sb = lambda n, sh, dt: es.enter_context(nc.sbuf_tensor(n, sh, dt))
        self.h = sb("hT", [128, NKC, NT], F32)
        self.u = sb("uT", [128, NKC, NT], BF16)
        self.r1 = sb("r1", [128, NKC, NT], BF16)
        self.NW = 6
        self.w = [sb("w%d" % i, [128, 2048], BF16) for i in range(self.NW)]
        self.wb = [Buf("w%d" % i) for i in range(self.NW)]
        self.wsem = [self.P.dsem() for _ in range(self.NW)]
        self.wnext = 0
        self.sq = [sb("sq%d" % i, [128, 512], BF16) for i in range(2)]
        self.sqb = [Buf("sq%d" % i) for i in range(2)]
        self.sqn = 0
        self.sg = [sb("sg%d" % i, [128, 512], F32) for i in range(2)]
        self.sgb = [Buf("sg%d" % i) for i in range(2)]
        self.sgn = 0
        self.st = [sb("st%d" % i, [128, 512], F32) for i in range(3)]
        self.stb = [Buf("st%d" % i) for i in range(3)]
        self.ps = [es.enter_context(nc.psum_tensor("ps%d" % i, [128, 512], F32)) for i in range(8)]
        self.psb = [Buf("ps%d" % i) for i in range(8)]
        self.bankn = {"A": 0, "B": 0, "C": 0}
        self.hb = [[Buf("h%d_%d" % (c, t)) for t in range(5)] for c in range(NKC)]
        self.ub = [Buf("u%d" % t) for t in range(5)]
        self.r1b = [[Buf("r%d_%d" % (c, t)) for t in range(5)] for c in range(NKC)]
        self.ones = sb("ones", [128, 128], BF16)
        self.onesb = Buf("ones")
        self.ident = sb("ident", [128, 128], BF16)
        self.identb = Buf("ident")
        self.csem = self.P.dsem()
        self.next_gcol = None
        self.normed = None

    def hook(self, t):
        if self.next_gcol is not None:
            self.rmsnorm(t, self.next_gcol)

    def hooks_done(self):
        self.normed = self.next_gcol
        self.next_gcol = None

    def ensure_norm(self, gcol):
        if self.normed == gcol:
            self.normed = None
            return
        for t in range(len(self.tiles)):
            self.rmsnorm(t, gcol)

    def T(self, t):
        t0, n = self.tiles[t]
        self.P.strict = n < 256
        return t0, n

    def bank(self, grp):
        base = {"A": 0, "B": 2, "C": 4}[grp]
        i = base + (self.bankn[grp] % 2)
        self.bankn[grp] += 1
        return i

    def wslot(self):
        s = self.wnext % self.NW
        self.wnext += 1
        return s

    def load_w(self, dram_ap, L):
        s = self.wslot()
        wt = self.w[s]
        self.P.add("pool", lambda e, wt=wt, dram_ap=dram_ap, L=L: e.dma_start(out=wt[:, 0:L], in_=dram_ap),
                   writes=[self.wb[s]], dma=self.wsem[s])
        return s

    def load_const(self, sbuf_ap, dram_ap, buf, eng="pool"):
        self.P.add(eng, lambda e: e.dma_start(out=sbuf_ap, in_=dram_ap), writes=[buf], dma=self.csem)

    def rmsnorm(self, t, gcol):
        P = self.P
        t0, n = self.T(t)
        h, u, prm = self.h, self.u, self.prm
        msb, rsb = 6, 7
        for c in range(NKC):
            s = self.sqn % 2
            self.sqn += 1
            sq = self.sq[s]
            P.add("act", lambda e, sq=sq, c=c: e.activation(out=sq[:, 0:n], in_=h[:, c, t0:t0 + n], func=AF.Square),
                  reads=[self.hb[c][t]], writes=[self.sqb[s]])
            P.add("pe", lambda e, sq=sq, c=c: e.matmul(self.ps[msb][:, 0:n], lhsT=self.ones[:, :], rhs=sq[:, 0:n],
                                                       start=(c == 0), stop=(c == NKC - 1)),
                  reads=[self.sqb[s], self.onesb], writes=[self.psb[msb]])
        sd = self.st[0]
        P.add("act", lambda e: e.activation(out=sd[:, 0:n], in_=self.ps[msb][:, 0:n], func=AF.Sqrt,
                                            bias=prm[:, self.c_eps:self.c_eps + 1], scale=1.0 / D),
              reads=[self.psb[msb], self.prmb], writes=[self.stb[0]])
        P.add("dve", lambda e: e.reciprocal(out=self.ps[rsb][:, 0:n], in_=sd[:, 0:n]),
              reads=[self.stb[0]], writes=[self.psb[rsb]])
        for c in range(NKC):
            P.add("dve", lambda e, c=c: e.scalar_tensor_tensor(
                out=u[:, c, t0:t0 + n], in0=h[:, c, t0:t0 + n], scalar=prm[:, gcol + c:gcol + c + 1],
                in1=self.ps[rsb][:, 0:n], op0=ALU.mult, op1=ALU.mult),
                reads=[self.hb[c][t], self.psb[rsb], self.prmb], writes=[self.ub[t]])

    def ffn(self, wg, wu, wd, gcol):
        P = self.P
        h, u, r1 = self.h, self.u, self.r1
        nt = len(self.tiles)
        self.ensure_norm(gcol)
        for ti, (f0, nf) in enumerate(THIRDS):
            for fb in range(nf // 2):
                blk = f0 // 2 + fb
                sg_ = self.load_w(wg[blk], 2048)
                su_ = self.load_w(wu[blk], 2048)
                for fc in range(2):
                    fl = fb * 2 + fc
                    for t in range(nt):
                        t0, n = self.T(t)
                        ba, bb = self.bank("A"), self.bank("B")
                        for (bk, sl) in ((ba, sg_), (bb, su_)):
                            for kc in range(NKC):
                                P.add("pe", lambda e, bk=bk, sl=sl, kc=kc, fc=fc, t0=t0, n=n: e.matmul(
                                    self.ps[bk][:, 0:n], lhsT=self.w[sl][:, kc * 256 + fc * 128: kc * 256 + fc * 128 + 128],
                                    rhs=u[:, kc, t0:t0 + n], start=(kc == 0), stop=(kc == NKC - 1)),
                                    reads=[self.wb[sl], self.ub[t]], writes=[self.psb[bk]])
                        s = self.sgn % 2
                        self.sgn += 1
                        sgt = self.sg[s]
                        P.add("act", lambda e, sgt=sgt, ba=ba, n=n: e.activation(out=sgt[:, 0:n], in_=self.ps[ba][:, 0:n], func=AF.Silu),
                              reads=[self.psb[ba]], writes=[self.sgb[s]])
                        P.add("dve", lambda e, sgt=sgt, bb=bb, fl=fl, t0=t0, n=n: e.tensor_tensor(
                            out=r1[:, fl, t0:t0 + n], in0=sgt[:, 0:n], in1=self.ps[bb][:, 0:n], op=ALU.mult),
                            reads=[self.sgb[s], self.psb[bb]], writes=[self.r1b[fl][t]])
            if ti == len(THIRDS) - 1:
                L = nf * 256
                sds = [self.load_w(wd[ti, dblk][:, 0:L], L) for dblk in range(4)]
                prev_t = None
                for t in range(nt):
                    t0, n = self.T(t)
                    for d in range(NKC):
                        sd_, dc = sds[d // 2], d % 2
                        bc = self.bank("C")
                        for fl in range(nf):
                            P.add("pe", lambda e, bc=bc, sd_=sd_, fl=fl, dc=dc, t0=t0, n=n: e.matmul(
                                self.ps[bc][:, 0:n], lhsT=self.w[sd_][:, fl * 256 + dc * 128: fl * 256 + dc * 128 + 128],
                                rhs=r1[:, fl, t0:t0 + n], start=(fl == 0), stop=(fl == nf - 1)),
                                reads=[self.wb[sd_], self.r1b[fl][t]], writes=[self.psb[bc]])
                        P.add("dve", lambda e, bc=bc, d=d, t0=t0, n=n: e.scalar_tensor_tensor(
                            out=h[:, d, t0:t0 + n], in0=self.ps[bc][:, 0:n], scalar=0.5, in1=h[:, d, t0:t0 + n],
                            op0=ALU.mult, op1=ALU.add),
                            reads=[self.psb[bc], self.hb[d][t]], writes=[self.hb[d][t]])
                    if prev_t is not None:
                        self.hook(prev_t)
                    prev_t = t
                self.hook(prev_t)
                self.hooks_done()
                continue
            for dblk in range(4):
                L = nf * 256
                sd_ = self.load_w(wd[ti, dblk][:, 0:L], L)
                for dc in range(2):
                    d = dblk * 2 + dc
                    for t in range(nt):
                        t0, n = self.T(t)
                        bc = self.bank("C")
                        for fl in range(nf):
                            P.add("pe", lambda e, bc=bc, sd_=sd_, fl=fl, dc=dc, t0=t0, n=n: e.matmul(
                                self.ps[bc][:, 0:n], lhsT=self.w[sd_][:, fl * 256 + dc * 128: fl * 256 + dc * 128 + 128],
                                rhs=r1[:, fl, t0:t0 + n], start=(fl == 0), stop=(fl == nf - 1)),
                                reads=[self.wb[sd_], self.r1b[fl][t]], writes=[self.psb[bc]])
                        P.add("dve", lambda e, bc=bc, d=d, t0=t0, n=n: e.scalar_tensor_tensor(
                            out=h[:, d, t0:t0 + n], in0=self.ps[bc][:, 0:n], scalar=0.5, in1=h[:, d, t0:t0 + n],
                            op0=ALU.mult, op1=ALU.add),
                            reads=[self.psb[bc], self.hb[d][t]], writes=[self.hb[d][t]])

    def outproj(self, wo, src, srcb):
        P = self.P
        h = self.h
        nt = len(self.tiles)
        sos = [self.load_w(wo[dblk], 2048) for dblk in range(4)]
        prev_t = None
        for t in range(nt):
            t0, n = self.T(t)
            for d in range(NKC):
                so, dc = sos[d // 2], d % 2
                bc = self.bank("C")
                for kc in range(NKC):
                    P.add("pe", lambda e, bc=bc, so=so, kc=kc, dc=dc, t0=t0, n=n: e.matmul(
                        self.ps[bc][:, 0:n], lhsT=self.w[so][:, kc * 256 + dc * 128: kc * 256 + dc * 128 + 128],
                        rhs=src[:, kc, t0:t0 + n], start=(kc == 0), stop=(kc == NKC - 1)),
                        reads=[self.wb[so], srcb(kc, t)], writes=[self.psb[bc]])
                P.add("dve", lambda e, bc=bc, d=d, t0=t0, n=n: e.tensor_tensor(
                    out=h[:, d, t0:t0 + n], in0=self.ps[bc][:, 0:n], in1=h[:, d, t0:t0 + n], op=ALU.add),
                    reads=[self.psb[bc], self.hb[d][t]], writes=[self.hb[d][t]])
            if prev_t is not None:
                self.hook(prev_t)
            prev_t = t
        self.hook(prev_t)
        self.hooks_done()

    def proj(self, bk, sl, off, t):
        t0, n = self.T(t)
        for kc in range(NKC):
            self.P.add("pe", lambda e, kc=kc: e.matmul(
                self.ps[bk][:, 0:n], lhsT=self.w[sl][:, kc * 256 + off: kc * 256 + off + 128],
                rhs=self.u[:, kc, t0:t0 + n], start=(kc == 0), stop=(kc == NKC - 1)),
                reads=[self.wb[sl], self.ub[t]], writes=[self.psb[bk]])

    def mixer_a(self, win, wo, gcol, ccol):
        P = self.P
        prm, r1 = self.prm, self.r1
        nt = len(self.tiles)
        es, nc = self.es, self.nc
        if not hasattr(self, "cv"):
            self.cv = [es.enter_context(nc.sbuf_tensor("cv%d" % i, [128, 514], F32)) for i in range(2)]
            self.cvb = [Buf("cv%d" % i) for i in range(2)]
            self.acc = es.enter_context(nc.sbuf_tensor("cacc", [128, 512], F32))
            self.accb = Buf("cacc")
        self.ensure_norm(gcol)
        cvn = 0
        for jp in range(4):
            sb_ = self.load_w(win[jp], 2048)
            sc_ = self.load_w(win[4 + jp], 2048)
            sv_ = self.load_w(win[8 + jp], 2048)
            for jc in range(2):
                j = jp * 2 + jc
                wc = ccol + j * 3
                for t in range(nt):
                    t0, n = self.T(t)
                    ba, bb, bc = self.bank("A"), self.bank("B"), self.bank("C")
                    self.proj(ba, sb_, jc * 128, t)
                    self.proj(bb, sc_, jc * 128, t)
                    self.proj(bc, sv_, jc * 128, t)
                    cur, prv = self.cv[cvn % 2], self.cv[(cvn + 1) % 2]
                    curb, prvb = self.cvb[cvn % 2], self.cvb[(cvn + 1) % 2]
                    cvn += 1
                    s = self.sgn % 2
                    self.sgn += 1
                    sgt = self.sg[s]
                    P.add("act", lambda e, sgt=sgt, bb=bb, n=n: e.activation(out=sgt[:, 0:n], in_=self.ps[bb][:, 0:n], func=AF.Copy),
                          reads=[self.psb[bb]], writes=[self.sgb[s]])
                    if t == 0:
                        P.add("dve", lambda e, cur=cur: e.memset(cur[:, 0:2], 0.0), writes=[curb])
                    else:
                        pn = self.tiles[t - 1][1]
                        st_ = P.strict
                        P.strict = True
                        P.add("dve", lambda e, cur=cur, prv=prv, pn=pn: e.tensor_copy(out=cur[:, 0:2], in_=prv[:, pn:pn + 2]),
                              reads=[prvb], writes=[curb])
                        P.strict = st_
                    P.add("dve", lambda e, cur=cur, sgt=sgt, bc=bc, n=n: e.tensor_tensor(
                        out=cur[:, 2:2 + n], in0=sgt[:, 0:n], in1=self.ps[bc][:, 0:n], op=ALU.mult),
                        reads=[self.sgb[s], self.psb[bc]], writes=[curb])
                    acc = self.acc
                    P.add("dve", lambda e, cur=cur, n=n, wc=wc: e.tensor_scalar(
                        out=acc[:, 0:n], in0=cur[:, 2:2 + n], scalar1=prm[:, wc + 2:wc + 3], scalar2=None, op0=ALU.mult),
                        reads=[curb, self.prmb], writes=[self.accb])
                    P.add("dve", lambda e, cur=cur, n=n, wc=wc: e.scalar_tensor_tensor(
                        out=acc[:, 0:n], in0=cur[:, 1:1 + n], scalar=prm[:, wc + 1:wc + 2], in1=acc[:, 0:n],
                        op0=ALU.mult, op1=ALU.add), reads=[curb], writes=[self.accb])
                    P.add("dve", lambda e, cur=cur, n=n, wc=wc: e.scalar_tensor_tensor(
                        out=acc[:, 0:n], in0=cur[:, 0:n], scalar=prm[:, wc:wc + 1], in1=acc[:, 0:n],
                        op0=ALU.mult, op1=ALU.add), reads=[curb], writes=[self.accb])
                    P.add("dve", lambda e, n=n, ba=ba, j=j, t0=t0: e.tensor_tensor(
                        out=r1[:, j, t0:t0 + n], in0=acc[:, 0:n], in1=self.ps[ba][:, 0:n], op=ALU.mult),
                        reads=[self.accb, self.psb[ba]], writes=[self.r1b[j][t]])
        self.outproj(wo, r1, lambda kc, t: self.r1b[kc][t])

    def mixer_b(self, win, wo, gcol, ccol, bcol, lgcol, lbcol):
        P = self.P
        prm, r1, u = self.prm, self.r1, self.u
        nt = len(self.tiles)
        es, nc = self.es, self.nc
        self.gl = [es.enter_context(nc.sbuf_tensor("gl%d" % i, [128, 542], BF16)) for i in range(2)]
        self.glb = [Buf("gl%d" % i) for i in range(2)]
        self.dg = es.enter_context(nc.sbuf_tensor("dg", [128, 31, 128], BF16))
        self.dgb = Buf("dg")
        self.zq = es.enter_context(nc.sbuf_tensor("zq", [128, 512], F32))
        self.zqb = Buf("zq")
        self.ensure_norm(gcol)
        gn = 0
        for jp in range(4):
            sa_ = self.load_w(win[jp], 2048)
            sg_ = self.load_w(win[4 + jp], 2048)
            for jc in range(2):
                j = jp * 2 + jc
                wc = ccol + j * 31
                for k in range(31):
                    P.add("pool", lambda e, k=k, wc=wc: e.tensor_scalar(
                        out=self.dg[:, k, :], in0=self.ident[:, :], scalar1=prm[:, wc + k:wc + k + 1], scalar2=None, op0=ALU.mult),
                        reads=[self.identb, self.prmb], writes=[self.dgb])
                for t in range(nt):
                    t0, n = self.T(t)
                    ba, bb, bc = self.bank("A"), self.bank("B"), self.bank("C")
                    self.proj(ba, sa_, jc * 128, t)
                    self.proj(bb, sg_, jc * 128, t)
                    cur, prv = self.gl[gn % 2], self.gl[(gn + 1) % 2]
                    curb, prvb = self.glb[gn % 2], self.glb[(gn + 1) % 2]
                    gn += 1
                    s = self.sgn % 2
                    self.sgn += 1
                    sgt = self.sg[s]
                    P.add("act", lambda e, sgt=sgt, bb=bb, n=n: e.activation(out=sgt[:, 0:n], in_=self.ps[bb][:, 0:n], func=AF.Sigmoid),
                          reads=[self.psb[bb]], writes=[self.sgb[s]])
                    if t == 0:
                        P.add("dve", lambda e, cur=cur: e.memset(cur[:, 0:30], 0.0), writes=[curb])
                    else:
                        pn = self.tiles[t - 1][1]
                        st_ = P.strict
                        P.strict = True
                        P.add("dve", lambda e, cur=cur, prv=prv, pn=pn: e.tensor_copy(out=cur[:, 0:30], in_=prv[:, pn:pn + 30]),
                              reads=[prvb], writes=[curb])
                        P.strict = st_
                    P.add("dve", lambda e, cur=cur, sgt=sgt, ba=ba, n=n: e.tensor_tensor(
                        out=cur[:, 30:30 + n], in0=sgt[:, 0:n], in1=self.ps[ba][:, 0:n], op=ALU.mult),
                        reads=[self.sgb[s], self.psb[ba]], writes=[curb])
                    for k in range(31):
                        P.add("pe", lambda e, k=k, bc=bc, cur=cur, n=n: e.matmul(
                            self.ps[bc][:, 0:n], lhsT=self.dg[:, k, :], rhs=cur[:, k:k + n], start=(k == 0), stop=(k == 30)),
                            reads=[self.dgb, curb], writes=[self.psb[bc]])
                    P.add("act", lambda e, bc=bc, j=j, t0=t0, n=n: e.activation(
                        out=r1[:, j, t0:t0 + n], in_=self.ps[bc][:, 0:n], func=AF.Identity,
                        bias=prm[:, bcol + j:bcol + j + 1], scale=1.0),
                        reads=[self.psb[bc], self.prmb], writes=[self.r1b[j][t]])
        for t in range(nt):
            t0, n = self.T(t)
            for c in range(NKC):
                P.add("pe", lambda e, c=c, t0=t0, n=n: e.matmul(self.ps[6][:, 0:n], lhsT=self.ones[:, :], rhs=r1[:, c, t0:t0 + n],
                                                                start=(c == 0), stop=(c == NKC - 1)),
                      reads=[self.r1b[c][t], self.onesb], writes=[self.psb[6]])
            for c in range(NKC):
                s = self.sqn % 2
                self.sqn += 1
                sq = self.sq[s]
                P.add("act", lambda e, sq=sq, c=c, t0=t0, n=n: e.activation(out=sq[:, 0:n], in_=r1[:, c, t0:t0 + n], func=AF.Square),
                      reads=[self.r1b[c][t]], writes=[self.sqb[s]])
                P.add("pe", lambda e, sq=sq, c=c, n=n: e.matmul(self.ps[7][:, 0:n], lhsT=self.ones[:, :], rhs=sq[:, 0:n],
                                                                start=(c == 0), stop=(c == NKC - 1)),
                      reads=[self.sqb[s], self.onesb], writes=[self.psb[7]])
            mean, var, sd = self.st[1], self.st[2], self.st[0]
            P.add("act", lambda e, n=n: e.activation(out=mean[:, 0:n], in_=self.ps[6][:, 0:n], func=AF.Copy, scale=1.0 / D),
                  reads=[self.psb[6]], writes=[self.stb[1]])
            P.add("dve", lambda e, n=n: e.tensor_tensor(out=var[:, 0:n], in0=mean[:, 0:n], in1=mean[:, 0:n], op=ALU.mult),
                  reads=[self.stb[1]], writes=[self.stb[2]])
            P.add("dve", lambda e, n=n: e.scalar_tensor_tensor(out=var[:, 0:n], in0=self.ps[7][:, 0:n], scalar=1.0 / D, in1=var[:, 0:n],
                                                               op0=ALU.mult, op1=ALU.subtract),
                  reads=[self.psb[7], self.stb[2]], writes=[self.stb[2]])
            P.add("act", lambda e, n=n: e.activation(out=sd[:, 0:n], in_=var[:, 0:n], func=AF.Sqrt,
                                                     bias=prm[:, self.c_eps:self.c_eps + 1], scale=1.0),
                  reads=[self.stb[2], self.prmb], writes=[self.stb[0]])
            P.add("dve", lambda e, n=n: e.reciprocal(out=self.ps[7][:, 0:n], in_=sd[:, 0:n]),
                  reads=[self.stb[0]], writes=[self.psb[7]])
            for c in range(NKC):
                zq = self.zq
                P.add("dve", lambda e, c=c, t0=t0, n=n: e.tensor_tensor(out=zq[:, 0:n], in0=r1[:, c, t0:t0 + n], in1=mean[:, 0:n], op=ALU.subtract),
                      reads=[self.r1b[c][t], self.stb[1]], writes=[self.zqb])
                P.add("dve", lambda e, c=c, n=n: e.scalar_tensor_tensor(out=zq[:, 0:n], in0=zq[:, 0:n], scalar=prm[:, lgcol + c:lgcol + c + 1],
                                                                   in1=self.ps[7][:, 0:n], op0=ALU.mult, op1=ALU.mult),
                      reads=[self.psb[7], self.prmb], writes=[self.zqb])
                P.add("act", lambda e, c=c, t0=t0, n=n: e.activation(out=u[:, c, t0:t0 + n], in_=zq[:, 0:n], func=AF.Silu,
                                                                bias=prm[:, lbcol + c:lbcol + c + 1], scale=1.0),
                      reads=[self.zqb, self.prmb], writes=[self.ub[t]])
        self.outproj(wo, u, lambda kc, t: self.ub[t])

    def finish(self):
        self.P.emit(self.nc, self.es)
        self.es.close()
        return self.nc


def tile_in(w, n=256):
    K, N = w.shape
    kc = K // 128
    return np.ascontiguousarray(w.reshape(kc, 128, N // n, n).transpose(2, 1, 0, 3).reshape(N // n, 128, kc * n))


def tile_down(wd):
    out = np.zeros((3, 4, 128, 2048), np.float32)
    for ti, (f0, nf) in enumerate(THIRDS):
        blk = wd[f0 * 128:(f0 + nf) * 128]
        out[ti, :, :, :nf * 256] = blk.reshape(nf, 128, 4, 256).transpose(2, 1, 0, 3).reshape(4, 128, nf * 256)
    return out


def colvec(v):
    return np.ascontiguousarray(v.reshape(NKC, 128).T)


class Params:
    def __init__(self):
        self.cols = []
        self.n = 0

    def add(self, arr):
        o = self.n
        self.cols.append(np.asarray(arr, np.float32))
        self.n += arr.shape[1]
        return o

    def build(self):
        return np.ascontiguousarray(np.concatenate(self.cols, axis=1))


def setup_consts(k, npr):
    nc, P = k.nc, k.P
    k.prm_d = nc.dram_tensor("prm", [128, npr], F32, kind="ExternalInput").ap()
    k.prm = k.es.enter_context(nc.sbuf_tensor("prm_sb", [128, npr], F32))
    k.prmb = Buf("prm")
    k.prmsem = P.dsem()
    P.add("sp", lambda e: e.dma_start(out=k.prm[:, :], in_=k.prm_d[:, :]), writes=[k.prmb], dma=k.prmsem)
    k.cst_d = nc.dram_tensor("cst", [128, 384], F32, kind="ExternalInput").ap()
    s1, s2, s3 = P.dsem(), P.dsem(), P.dsem()
    k.bones = k.es.enter_context(nc.sbuf_tensor("bones", [128, 128], BF16))
    k.bonesb = Buf("bones")
    P.add("pool", lambda e: e.dma_start(out=k.ones[:, :], in_=k.cst_d[:, 0:128]), writes=[k.onesb], dma=s1)
    P.add("pool", lambda e: e.dma_start(out=k.ident[:, :], in_=k.cst_d[:, 128:256]), writes=[k.identb], dma=s2)
    P.add("pool", lambda e: e.dma_start(out=k.bones[:, :], in_=k.cst_d[:, 256:384]), writes=[k.bonesb], dma=s3)


def consts_np():
    bo = np.zeros((128, 128), np.float32)
    bo[:64, :64] = 1.0
    bo[64:, 64:] = 1.0
    return np.ascontiguousarray(np.concatenate([np.ones((128, 128), np.float32), np.eye(128, dtype=np.float32), bo], axis=1))


def build_l1(stages, npr, cols):
    k = KB(pre=32)
    nc, P = k.nc, k.P
    NT = k.NT
    k.c_eps = cols["eps"]
    setup_consts(k, npr)
    xT = nc.dram_tensor("xT", [D, NT], F32, kind="ExternalInput").ap()
    hout = nc.dram_tensor("hout", [D, NT], F32, kind="ExternalOutput").ap()
    wts = {}

    def din(name, shape):
        wts[name] = nc.dram_tensor(name, shape, F32, kind="ExternalInput").ap()
        return wts[name]

    lsem = P.dsem()
    for c in range(NKC):
        P.add("sp", lambda e, c=c: e.dma_start(out=k.h[:, c, :], in_=xT[c * 128:(c + 1) * 128, :]),
              writes=k.hb[c], dma=lsem)
    for c in range(NKC):
        for t in range(5):
            k.hb[c][t].w = (("dma", lsem), P.dma_cnt[lsem])
    seq = []
    for L in range(3):
        seq.append(("ffa", L, cols["fn%da" % L]))
        seq.append(("mix", L, cols["mn%d" % L]))
        if L < 2:
            seq.append(("ffb", L, cols["fn%db" % L]))
    seq = seq[:stages]
    for i, (kind, L, gc) in enumerate(seq):
        k.next_gcol = seq[i + 1][2] if i + 1 < len(seq) else None
        if kind == "ffa":
            k.ffn(din("wg%da" % L, [11, 128, 2048]), din("wu%da" % L, [11, 128, 2048]), din("wd%da" % L, [3, 4, 128, 2048]), gc)
        elif kind == "ffb":
            k.ffn(din("wg%db" % L, [11, 128, 2048]), din("wu%db" % L, [11, 128, 2048]), din("wd%db" % L, [3, 4, 128, 2048]), gc)
        elif L == 0:
            k.mixer_a(din("ain0", [12, 128, 2048]), din("aout0", [4, 128, 2048]), gc, cols["aconv0"])
        elif L == 1:
            k.mixer_b(din("bin", [8, 128, 2048]), din("bout", [4, 128, 2048]), gc, cols["bconv"], cols["bcb"],
                      cols["blg"], cols["blb"])
        else:
            k.next_gcol = None
            attn_proj(k, din, cols)
    osem = P.dsem()
    outb = Buf("out")
    for c in range(NKC):
        P.add("sp", lambda e, c=c: e.dma_start(out=hout[c * 128:(c + 1) * 128, :], in_=k.h[:, c, :]),
              reads=[k.hb[c][t] for t in range(5)], writes=[outb], dma=osem)
    fin = k.es.enter_context(nc.sbuf_tensor("fin", [128, 1], F32))
    P.add("dve", lambda e: e.memset(fin[:, :], 0.0), reads=[outb], writes=getattr(k, "extra_out", []))
    return k.finish(), list(wts.keys())


def attn_proj(k, din, cols):
    nc, P, es = k.nc, k.P, k.es
    NT = k.NT
    prm, u = k.prm, k.u
    nt = len(k.tiles)
    cin = din("cin", [12, 128, 2048])
    wf_d = din("cwf", [128, 128])
    bfb_d = din("bfb", [128, 64])
    qT = nc.dram_tensor("qT", [D, NT], BF16, kind="ExternalOutput").ap()
    kT = nc.dram_tensor("kT", [D, NT], BF16, kind="ExternalOutput").ap()
    vo = nc.dram_tensor("vo", [NT, D], BF16, kind="ExternalOutput").ap()
    lfo = nc.dram_tensor("lfo", [NT, NH], F32, kind="ExternalOutput").ap()
    og = [es.enter_context(nc.sbuf_tensor("og%d" % i, [128, 512], BF16)) for i in range(2)]
    ogb = [Buf("og%d" % i) for i in range(2)]
    ogs = [P.dsem() for _ in range(2)]
    lst = [es.enter_context(nc.sbuf_tensor("lst%d" % i, [128, 64], F32)) for i in range(2)]
    lstb = [Buf("lst%d" % i) for i in range(2)]
    lss = [P.dsem() for _ in range(2)]
    bfb = es.enter_context(nc.sbuf_tensor("bfb_sb", [128, 64], F32))
    bfbb = Buf("bfb")
    P.add("sp", lambda e: e.dma_start(out=bfb[:, :], in_=bfb_d[:, :]), writes=[bfbb], dma=P.dsem())
    outs = []
    k.ensure_norm(cols["mn2"])
    ogn = 0
    for which, dst, gcol, scl, bcol in (("q", qT, cols["qg"], 1.0, cols["eps64"]), ("k", kT, cols["kg"], 1.0 / HD, cols["eps"])):
        base = 0 if which == "q" else 4
        for jp in range(4):
            sl = k.load_w(cin[base + jp], 2048)
            for jc in range(2):
                j = jp * 2 + jc
                for t in range(nt):
                    t0, n = k.T(t)
                    ba = k.bank("A")
                    k.proj(ba, sl, jc * 128, t)
                    s = k.sqn % 2
                    k.sqn += 1
                    sq = k.sq[s]
                    P.add("act", lambda e, sq=sq, ba=ba, n=n: e.activation(out=sq[:, 0:n], in_=k.ps[ba][:, 0:n], func=AF.Square),
                          reads=[k.psb[ba]], writes=[k.sqb[s]])
                    P.add("pe", lambda e, sq=sq, n=n: e.matmul(k.ps[6][:, 0:n], lhsT=k.bones[:, :], rhs=sq[:, 0:n], start=True, stop=True),
                          reads=[k.sqb[s], k.bonesb], writes=[k.psb[6]])
                    P.add("act", lambda e, n=n, scl=scl, bcol=bcol: e.activation(out=k.st[0][:, 0:n], in_=k.ps[6][:, 0:n], func=AF.Sqrt,
                                                                       bias=prm[:, bcol:bcol + 1], scale=scl),
                          reads=[k.psb[6], k.prmb], writes=[k.stb[0]])
                    P.add("dve", lambda e, n=n: e.reciprocal(out=k.st[1][:, 0:n], in_=k.st[0][:, 0:n]),
                          reads=[k.stb[0]], writes=[k.stb[1]])
                    o = ogn % 2
                    ogn += 1
                    P.add("dve", lambda e, o=o, ba=ba, n=n, gcol=gcol: e.scalar_tensor_tensor(
                        out=og[o][:, 0:n], in0=k.ps[ba][:, 0:n], scalar=prm[:, gcol:gcol + 1], in1=k.st[1][:, 0:n],
                        op0=ALU.mult, op1=ALU.mult), reads=[k.psb[ba], k.stb[1], k.prmb], writes=[ogb[o]])
                    P.add("sp", lambda e, o=o, dst=dst, j=j, t0=t0, n=n: e.dma_start(out=dst[j * 128:(j + 1) * 128, t0:t0 + n], in_=og[o][:, 0:n]),
                          reads=[ogb[o]], dma=ogs[o])
    for vb in range(4):
        sl = k.load_w(cin[8 + vb], 2048)
        for t in range(nt):
            t0, n = k.T(t)
            subs = [(0, n)] if n <= 128 else [(a * 128, 128) for a in range(n // 128)]
            for g0 in range(0, len(subs), 2):
                grp = subs[g0:g0 + 2]
                bb = k.bank("B")
                for gi, (so, m) in enumerate(grp):
                    for kc in range(NKC):
                        P.add("pe", lambda e, bb=bb, gi=gi, so=so, m=m, kc=kc, sl=sl, t0=t0: e.matmul(
                            k.ps[bb][0:m, gi * 256:(gi + 1) * 256], lhsT=u[:, kc, t0 + so:t0 + so + m],
                            rhs=k.w[sl][:, kc * 256:(kc + 1) * 256], start=(kc == 0), stop=(kc == NKC - 1)),
                            reads=[k.wb[sl], k.ub[t]], writes=[k.psb[bb]])
                m = grp[0][1]
                ng = len(grp)
                o = ogn % 2
                ogn += 1
                P.add("act", lambda e, o=o, bb=bb, m=m, ng=ng: e.activation(out=og[o][0:m, 0:ng * 256], in_=k.ps[bb][0:m, 0:ng * 256], func=AF.Copy),
                      reads=[k.psb[bb]], writes=[ogb[o]])
                r0 = t0 + grp[0][0]
                if ng == 2:
                    P.add("sp", lambda e, o=o, r0=r0, vb=vb: e.dma_start(
                        out=vo[r0:r0 + 256, vb * 256:(vb + 1) * 256].rearrange("(g p) c -> p g c", p=128),
                        in_=og[o][:, 0:512].rearrange("p (g c) -> p g c", g=2)), reads=[ogb[o]], dma=ogs[o])
                else:
                    P.add("sp", lambda e, o=o, r0=r0, vb=vb, m=m: e.dma_start(
                        out=vo[r0:r0 + m, vb * 256:(vb + 1) * 256], in_=og[o][0:m, 0:256]), reads=[ogb[o]], dma=ogs[o])
    wfs = k.wslot()
    P.add("pool", lambda e: e.dma_start(out=k.w[wfs][:, 0:128], in_=wf_d[:, :]), writes=[k.wb[wfs]], dma=k.wsem[wfs])
    ln = 0
    for t in range(nt):
        t0, n = k.T(t)
        subs = [(0, n)] if n <= 128 else [(a * 128, 128) for a in range(n // 128)]
        bb = k.bank("B")
        for gi, (so, m) in enumerate(subs):
            for kc in range(NKC):
                P.add("pe", lambda e, bb=bb, gi=gi, so=so, m=m, kc=kc, t0=t0: e.matmul(
                    k.ps[bb][0:m, gi * 16:(gi + 1) * 16], lhsT=u[:, kc, t0 + so:t0 + so + m],
                    rhs=k.w[wfs][:, kc * 16:(kc + 1) * 16], start=(kc == 0), stop=(kc == NKC - 1)),
                    reads=[k.wb[wfs], k.ub[t]], writes=[k.psb[bb]])
        m = subs[0][1]
        ng = len(subs)
        o = ln % 2
        ln += 1
        W_ = ng * 16
        P.strict = True
        P.add("dve", lambda e, o=o, bb=bb, m=m, W_=W_: e.tensor_tensor(out=lst[o][0:m, 0:W_], in0=k.ps[bb][0:m, 0:W_], in1=bfb[0:m, 0:W_], op=ALU.add),
              reads=[k.psb[bb], bfbb], writes=[lstb[o]])
        P.add("act", lambda e, o=o, m=m, W_=W_: e.activation(out=lst[o][0:m, 0:W_], in_=lst[o][0:m, 0:W_], func=AF.Exp, scale=-1.0),
              reads=[lstb[o]], writes=[lstb[o]])
        P.add("act", lambda e, o=o, m=m, W_=W_: e.activation(out=lst[o][0:m, 0:W_], in_=lst[o][0:m, 0:W_], func=AF.Ln,
                                                        bias=prm[0:m, cols["one"]:cols["one"] + 1], scale=1.0),
              reads=[lstb[o], k.prmb], writes=[lstb[o]])
        P.add("dve", lambda e, o=o, m=m, W_=W_: e.tensor_scalar(out=lst[o][0:m, 0:W_], in0=lst[o][0:m, 0:W_], scalar1=-1.0, scalar2=None, op0=ALU.mult),
              reads=[lstb[o]], writes=[lstb[o]])
        if ng > 1:
            P.add("sp", lambda e, o=o, t0=t0, n=n, ng=ng: e.dma_start(
                out=lfo[t0:t0 + n, :].rearrange("(g p) c -> p g c", p=128),
                in_=lst[o][:, 0:ng * 16].rearrange("p (g c) -> p g c", g=ng)), reads=[lstb[o]], dma=lss[o])
        else:
            P.add("sp", lambda e, o=o, t0=t0, m=m: e.dma_start(out=lfo[t0:t0 + m, :], in_=lst[o][0:m, 0:16]), reads=[lstb[o]], dma=lss[o])
    k.extra_out = ogb + lstb


def prep_common(inputs):
    f = lambda a: np.asarray(a, np.float32)
    return {kk: f(v) for kk, v in inputs.items()}


def run_l1(inp, stages):
    pr = Params()
    cols = {}
    cols["eps"] = pr.add(np.full((128, 1), EPS, np.float32))
    cols["eps64"] = pr.add(np.full((128, 1), HD * EPS, np.float32))
    cols["one"] = pr.add(np.ones((128, 1), np.float32))
    for L in range(3):
        cols["fn%da" % L] = pr.add(colvec(inp["ffn_norm"][L, 0]))
        cols["fn%db" % L] = pr.add(colvec(inp["ffn_norm"][L, 1]))
        cols["mn%d" % L] = pr.add(colvec(inp["mix_norm"][L]))
    ac = inp["a_conv"][0]
    cols["aconv0"] = pr.add(np.concatenate([ac[:, j * 128:(j + 1) * 128].T for j in range(NKC)], axis=1))
    bc = inp["b_conv"][0]
    cols["bconv"] = pr.add(np.concatenate([bc[:, j * 128:(j + 1) * 128].T for j in range(NKC)], axis=1))
    cols["bcb"] = pr.add(colvec(inp["b_conv_bias"][0]))
    cols["blg"] = pr.add(colvec(inp["b_ln_g"][0]))
    cols["blb"] = pr.add(colvec(inp["b_ln_b"][0]))
    cols["qg"] = pr.add(np.tile(inp["c_q_norm"][0], 2)[:, None])
    cols["kg"] = pr.add(np.tile(inp["c_k_norm"][0], 2)[:, None])
    prm = pr.build()
    nc, wnames = build_l1(stages, prm.shape[1], cols)
    W = {}
    for L in range(3):
        for s, si in (("a", 0), ("b", 1)):
            if "wg%d%s" % (L, s) in wnames:
                W["wg%d%s" % (L, s)] = tile_in(inp["ffn_w_gate"][L, si])
                W["wu%d%s" % (L, s)] = tile_in(inp["ffn_w_up"][L, si])
                W["wd%d%s" % (L, s)] = tile_down(inp["ffn_w_down"][L, si])
    if "ain0" in wnames:
        W["ain0"] = tile_in(inp["a_w_in"][0])
        W["aout0"] = tile_in(inp["a_w_out"][0])
    if "bin" in wnames:
        W["bin"] = tile_in(inp["b_w_in"][0])
        W["bout"] = tile_in(inp["b_w_out"][0])
    if "cin" in wnames:
        cw = inp["c_w_in"][0]
        W["cin"] = tile_in(np.ascontiguousarray(cw[:, :3 * D]))
        W["cwf"] = np.ascontiguousarray(cw[:, 3 * D:].reshape(NKC, 128, NH).transpose(1, 0, 2).reshape(128, NKC * NH))
        W["bfb"] = np.ascontiguousarray(np.broadcast_to(np.tile(inp["c_b_f"][0], 4)[None, :], (128, 64)))
    cst = consts_np()
    x, meta = inp["x"], inp["meta"]
    in_maps = []
    for c in range(NCORE):
        b, r = c // 4, c % 4
        if r == 0:
            tok = np.concatenate([np.zeros((16, D), np.float32), meta, x[b, 0:CH]], axis=0)
        else:
            tok = x[b, r * CH - 32:(r + 1) * CH]
        m = {"xT": np.ascontiguousarray(tok.T), "prm": prm, "cst": cst}
        m.update(W)
        in_maps.append(m)
    res = run_bass_kernel_spmd(nc, in_maps, core_ids=list(range(NCORE)))
    return res


NQ = CH + 16


def job_dims(g):
    hc = 3 - g
    return hc, hc * CH + NQ, 16 * hc + 17


def build_l2():
    nc = bass.Bass("TRN2", target_bir_lowering=False)
    P = Prog()
    es = ExitStack()
    sb = lambda n, sh, dt: es.enter_context(nc.sbuf_tensor(n, sh, dt))
    NKM, NKBM = job_dims(0)[1], job_dims(0)[2]
    Ka = [sb("Ka%d" % i, [128, NKM], BF16) for i in range(2)]
    Va = [sb("Va%d" % i, [128, NKBM * 128], BF16) for i in range(2)]
    Qa = [sb("Qa%d" % i, [128, NQ], BF16) for i in range(2)]
    Kb = [Buf("Ka%d" % i) for i in range(2)]
    Vb = [Buf("Va%d" % i) for i in range(2)]
    Qb = [Buf("Qa%d" % i) for i in range(2)]
    Ks = [P.dsem() for _ in range(2)]
    Vs = [P.dsem() for _ in range(2)]
    Qs = [P.dsem() for _ in range(2)]
    Gs = [P.dsem() for _ in range(2)]
    lfT = sb("lfT", [4, NKM], F32)
    lfb = Buf("lfT")
    zer = sb("zer", [4, NKM], F32)
    zerb = Buf("zer")
    cumT = sb("cumT", [4, NKM], F32)
    cumb = Buf("cumT")
    gq = sb("gq", [4, NQ], BF16)
    gqb = Buf("gq")
    nb = sb("nb", [128, NKBM * 4], F32)
    nbb = Buf("nb")
    PT = [sb("PT%d" % i, [128, 512], BF16) for i in range(3)]
    PTb = [Buf("PT%d" % i) for i in range(3)]
    tri = sb("tri", [128, 2048], BF16)
    trib = Buf("tri")
    identb_ = sb("identb", [128, 128], BF16)
    identbb = Buf("identb")
    identf = sb("identf", [128, 128], F32)
    identfb = Buf("identf")
    onesf = sb("onesf", [128, 128], F32)
    onesfb = Buf("onesf")
    rc = sb("rc", [128, 512], F32)
    rcb = Buf("rc")
    rbt = sb("rbt", [128, 512], F32)
    rbb = Buf("rbt")
    ost = [sb("ost%d" % i, [128, 512], BF16) for i in range(2)]
    ostb = [Buf("ost%d" % i) for i in range(2)]
    osts = [P.dsem() for _ in range(2)]
    ps = [es.enter_context(nc.psum_tensor("ps%d" % i, [128, 512], F32)) for i in range(8)]
    psb = [Buf("ps%d" % i) for i in range(8)]
    cst = nc.dram_tensor("cst2", [128, 2304], F32, kind="ExternalInput").ap()
    P.add("pool", lambda e: e.dma_start(out=tri[:, :], in_=cst[:, 0:2048]), writes=[trib], dma=P.dsem())
    P.add("pool", lambda e: e.dma_start(out=identb_[:, :], in_=cst[:, 2048:2176]), writes=[identbb], dma=P.dsem())
    P.add("sp", lambda e: e.dma_start(out=identf[:, :], in_=cst[:, 2048:2176]), writes=[identfb], dma=P.dsem())
    P.add("sp", lambda e: e.dma_start(out=onesf[:, :], in_=cst[:, 2176:2304]), writes=[onesfb], dma=P.dsem())
    P.add("pool", lambda e: e.memset(zer[:, :], 0.0), writes=[zerb])
    for i in range(2):
        P.add("dve", lambda e, i=i: e.memset(Ka[i][64:65, :], 1.0), writes=[Kb[i]])
    P.add("pool", lambda e: e.memset(Va[0][:, :].rearrange("p (k c) -> p k c", c=128)[:, :, 64:128], 1.0), writes=[Vb[0]])
    P.add("pool", lambda e: e.memset(Va[1][:, :].rearrange("p (k c) -> p k c", c=128)[:, :, 0:64], 1.0), writes=[Vb[1]])
    qtiles = [(0, 16)] + [(16 + 512 * i, 512) for i in range(4)]
    hn = 0
    sbn = 0
    ptn = 0
    accn = 0
    osn = 0
    outs = []
    lsem = P.dsem()
    jobs = []
    lds = {}
    for g in range(4):
        hc, NK, NKB = job_dims(g)
        Hb = 16 * hc
        qd = nc.dram_tensor("q%d" % g, [4, HD, NQ], BF16, kind="ExternalInput").ap()
        kd = nc.dram_tensor("k%d" % g, [4, HD, NK], BF16, kind="ExternalInput").ap()
        vd = nc.dram_tensor("v%d" % g, [4, 128, NKB * HD], BF16, kind="ExternalInput").ap()
        ld = nc.dram_tensor("l%d" % g, [4, NK], F32, kind="ExternalInput").ap()
        ao = nc.dram_tensor("ao%d" % g, [4 * HD, NQ], BF16, kind="ExternalOutput").ap()
        aob = Buf("ao%d" % g)
        outs.append(aob)
        kbt = [(kb * 128, 128) for kb in range(Hb)] + [(Hb * 128, 16)] + [(Hb * 128 + 16 + b * 128, 128) for b in range(16)]
        lds[g] = ld
        jobs.append((g, hc, NK, NKB, Hb, qd, kd, vd, ao, aob, kbt))
    heads = []
    for (g, hc, NK, NKB, Hb, qd, kd, vd, ao, aob, kbt) in jobs:
        for hl in range(4):
            heads.append((g, hl))
    jobd = {j[0]: j for j in jobs}

    def emit_loads(hi):
        g, hl = heads[hi]
        (_, hc, NK, NKB, Hb, qd, kd, vd, ao, aob, kbt) = jobd[g]
        sl = hi % 2
        P.add("sp", lambda e: e.dma_start(out=Qa[sl][0:64, :], in_=qd[hl]), writes=[Qb[sl]], dma=Qs[sl])
        P.add("sp", lambda e: e.dma_start(out=Ka[sl][0:64, 0:NK], in_=kd[hl]), writes=[Kb[sl]], dma=Ks[sl])
        vc0 = 0 if sl == 0 else 64
        hk = NKB // 2
        for (ka, kz) in ((0, hk), (hk, NKB)):
            P.add("sp", lambda e, ka=ka, kz=kz: e.dma_start(
                out=Va[sl][:, 0:NKB * 128].rearrange("p (k c) -> p k c", c=128)[:, ka:kz, vc0:vc0 + 64],
                in_=vd[hl].rearrange("p (k c) -> p k c", c=64)[:, ka:kz, :]), writes=[Vb[sl]], dma=Vs[sl])

    def emit_jobsetup(g):
        (_, hc, NK, NKB, Hb, qd, kd, vd, ao, aob, kbt) = jobd[g]
        ld = lds[g]
        P.add("sp", lambda e: e.dma_start(out=lfT[:, 0:NK], in_=ld[:, :]), writes=[lfb], dma=lsem)
        P.add("dve", lambda e: e.tensor_tensor_scan(out=cumT[:, 0:NK], data0=lfT[:, 0:NK], data1=zer[:, 0:NK], initial=0.0,
                                                    op0=ALU.add, op1=ALU.add),
              reads=[lfb, zerb], writes=[cumb])
        P.add("dve", lambda e: e.tensor_copy(out=gq[:, :], in_=cumT[:, NK - NQ:NK]), reads=[cumb], writes=[gqb])
        for kb, (k0, m) in enumerate(kbt):
            P.add("pe", lambda e, kb=kb, k0=k0, m=m: e.transpose(out=ps[6][0:m, kb * 4:kb * 4 + 4], in_=cumT[0:4, k0:k0 + m],
                                                              identity=identf[0:4, 0:4]),
                  reads=[cumb, identfb], writes=[psb[6]])
        P.add("dve", lambda e: e.tensor_scalar(out=nb[:, 0:NKB * 4], in0=ps[6][:, 0:NKB * 4], scalar1=-1.0, scalar2=None, op0=ALU.mult),
              reads=[psb[6]], writes=[nbb])

    units = []
    for hi, (g, hl) in enumerate(heads):
        (_, hc, NK, NKB, Hb, qd, kd, vd, ao, aob, kbt) = jobd[g]
        for ti, (t0, n) in enumerate(qtiles):
            blocks = [(kb, None) for kb in range(Hb)]
            if ti == 0:
                blocks.append((Hb, 0))
            else:
                blocks.append((Hb, None))
                for b in range(4 * (ti - 1)):
                    blocks.append((Hb + 1 + b, None))
                for b in range(4 * (ti - 1), 4 * ti):
                    blocks.append((Hb + 1 + b, (b - 4 * (ti - 1)) * 128))
            for bi, (kb, o) in enumerate(blocks):
                units.append(dict(hi=hi, g=g, hl=hl, ti=ti, t0=t0, n=n, kb=kb, o=o, bi=bi, nbk=len(blocks),
                                  first_of_head=(ti == 0 and bi == 0), first_of_job=(ti == 0 and bi == 0 and hl == 0)))
    state = dict(sbn=0, ptn=0, accn=0, osn=0)
    pending = []

    def emit_qk(un):
        hi, g, hl = un["hi"], un["g"], un["hl"]
        if un["first_of_job"]:
            emit_jobsetup(g)
        if un["first_of_head"]:
            if hi == 0:
                emit_loads(0)
            sl = hi % 2
            P.add("sp", lambda e: e.dma_start(out=Qa[sl][64:65, :], in_=gq[hl:hl + 1, :]), reads=[gqb], writes=[Qb[sl]], dma=Qs[sl])
            if hi + 1 < len(heads):
                emit_loads(hi + 1)
        (_, hc, NK, NKB, Hb, qd, kd, vd, ao, aob, kbt) = jobd[g]
        sl = hi % 2
        k0, m = kbt[un["kb"]]
        t0, n, o = un["t0"], un["n"], un["o"]
        if un["bi"] == 0:
            un["acc"] = 4 + (state["accn"] % 2)
            state["accn"] += 1
            state["curacc"] = un["acc"]
        else:
            un["acc"] = state["curacc"]
        sbk = state["sbn"] % 4
        state["sbn"] += 1
        un["sbk"] = sbk
        P.add("pe", lambda e: e.matmul(ps[sbk][0:m, 0:n], lhsT=Ka[sl][0:65, k0:k0 + m], rhs=Qa[sl][0:65, t0:t0 + n],
                                       start=True, stop=(o is None)),
              reads=[Kb[sl], Qb[sl]], writes=[psb[sbk]])
        if o is not None:
            oo = (o // 128) * 512
            P.add("pe", lambda e: e.matmul(ps[sbk][0:m, 0:n], lhsT=identb_[0:m, 0:m], rhs=tri[0:m, oo:oo + n], start=False, stop=True),
                  reads=[identbb, trib], writes=[psb[sbk]])

    def emit_pv(un):
        hi, g, hl = un["hi"], un["g"], un["hl"]
        (_, hc, NK, NKB, Hb, qd, kd, vd, ao, aob, kbt) = jobd[g]
        sl = hi % 2
        kb = un["kb"]
        k0, m = kbt[kb]
        t0, n = un["t0"], un["n"]
        sbk, acc = un["sbk"], un["acc"]
        pi = state["ptn"] % 3
        state["ptn"] += 1
        P.add("act", lambda e: e.activation(out=PT[pi][0:m, 0:n], in_=ps[sbk][0:m, 0:n], func=AF.Exp,
                                            bias=nb[0:m, kb * 4 + hl:kb * 4 + hl + 1], scale=1.0),
              reads=[psb[sbk], nbb], writes=[PTb[pi]])
        P.add("pe", lambda e: e.matmul(ps[acc][:, 0:n], lhsT=Va[sl][0:m, kb * 128:(kb + 1) * 128], rhs=PT[pi][0:m, 0:n],
                                       start=(un["bi"] == 0), stop=(un["bi"] == un["nbk"] - 1)),
              reads=[Vb[sl], PTb[pi]], writes=[psb[acc]])
        if un["bi"] == un["nbk"] - 1:
            even = (sl == 0)
            p0 = 64 if even else 0
            while pending:
                emit_fin(pending.pop(0)[1])
            P.add("dve", lambda e: e.reciprocal(out=rc[p0:p0 + 1, 0:n], in_=ps[acc][p0:p0 + 1, 0:n]),
                  reads=[psb[acc]], writes=[rcb])
            pending.append([4, un])

    def emit_fin(un):
        hi, g, hl = un["hi"], un["g"], un["hl"]
        (_, hc, NK, NKB, Hb, qd, kd, vd, ao, aob, kbt) = jobd[g]
        sl = hi % 2
        even = (sl == 0)
        p0 = 64 if even else 0
        q0 = 0 if even else 64
        t0, n, acc = un["t0"], un["n"], un["acc"]
        P.add("pe", lambda e: e.matmul(ps[7][:, 0:n], lhsT=onesf[p0:p0 + 1, 0:128], rhs=rc[p0:p0 + 1, 0:n], start=True, stop=True),
              reads=[rcb, onesfb], writes=[psb[7]])
        P.add("act", lambda e: e.activation(out=rbt[:, 0:n], in_=ps[7][:, 0:n], func=AF.Copy), reads=[psb[7]], writes=[rbb])
        oi = state["osn"] % 2
        state["osn"] += 1
        P.add("dve", lambda e: e.tensor_tensor(out=ost[oi][q0:q0 + 64, 0:n], in0=ps[acc][q0:q0 + 64, 0:n], in1=rbt[q0:q0 + 64, 0:n], op=ALU.mult),
              reads=[psb[acc], rbb], writes=[ostb[oi]])
        P.add("pool", lambda e: e.dma_start(out=ao[hl * 64:(hl + 1) * 64, t0:t0 + n], in_=ost[oi][q0:q0 + 64, 0:n]),
              reads=[ostb[oi]], writes=[aob], dma=osts[oi])

    LOOK = 2
    inflight = []

    def tick():
        for pe_ in pending:
            pe_[0] -= 1
        while pending and pending[0][0] <= 0:
            emit_fin(pending.pop(0)[1])

    for un in units:
        if un["first_of_head"]:
            while inflight:
                emit_pv(inflight.pop(0))
        emit_qk(un)
        inflight.append(un)
        if len(inflight) > LOOK:
            emit_pv(inflight.pop(0))
        tick()
    while inflight:
        emit_pv(inflight.pop(0))
    while pending:
        emit_fin(pending.pop(0)[1])
    fin = sb("fin", [128, 1], F32)
    P.add("dve", lambda e: e.memset(fin[:, :], 0.0), writes=ostb)
    P.emit(nc, es)
    es.close()
    return nc


def consts2_np():
    tri = np.zeros((128, 4, 512), np.float32)
    p = np.arange(128)[:, None]
    q = np.arange(512)[None, :]
    for o in range(4):
        tri[:, o, :] = np.where(o * 128 + p > q, NEG, 0.0)
    return np.ascontiguousarray(np.concatenate([tri.reshape(128, 2048), np.eye(128, dtype=np.float32),
                                                np.ones((128, 128), np.float32)], axis=1))


def job_of(c, g):
    return c // 4, 3 - g, (c + g) % 4


def run_l2(Qs, Ks, Vs, LF):
    nc = build_l2()
    cst = consts2_np()
    in_maps = []
    for c in range(NCORE):
        m = {"cst2": cst}
        for g in range(4):
            b, r, hg = job_of(c, g)
            hc, NK, NKB = job_dims(g)
            s = CH * r
            m["q%d" % g] = np.ascontiguousarray(Qs[b][hg * 256:(hg + 1) * 256, s:s + NQ].reshape(4, HD, NQ))
            m["k%d" % g] = np.ascontiguousarray(Ks[b][hg * 256:(hg + 1) * 256, 0:NK].reshape(4, HD, NK))
            Vh = Vs[b][0:NK, hg * 256:(hg + 1) * 256].reshape(NK, 4, HD)
            Hb = 16 * hc
            vt = np.zeros((4, NKB, 128, HD), Vh.dtype)
            vt[:, 0:Hb] = Vh[0:Hb * 128].reshape(Hb, 128, 4, HD).transpose(2, 0, 1, 3)
            vt[:, Hb, 0:16] = Vh[Hb * 128:Hb * 128 + 16].transpose(1, 0, 2)
            vt[:, Hb + 1:] = Vh[Hb * 128 + 16:].reshape(16, 128, 4, HD).transpose(2, 0, 1, 3)
            m["v%d" % g] = np.ascontiguousarray(vt.transpose(0, 2, 1, 3).reshape(4, 128, NKB * HD))
            m["l%d" % g] = np.ascontiguousarray(LF[b][0:NK, hg * 4:(hg + 1) * 4].T)
        in_maps.append(m)
    res = run_bass_kernel_spmd(nc, in_maps, core_ids=list(range(NCORE)))
    AO = [[None] * 4 for _ in range(2)]
    for b in range(2):
        for r in range(4):
            AO[b][r] = np.zeros((D, NQ), Qs[0].dtype)
    for c in range(NCORE):
        for g in range(4):
            b, r, hg = job_of(c, g)
            AO[b][r][hg * 256:(hg + 1) * 256] = np.asarray(res.results[c]["ao%d" % g])
    return AO


def build_l3(npr, cols):
    k = KB(pre=16)
    nc, P = k.nc, k.P
    NT = k.NT
    k.c_eps = cols["eps"]
    setup_consts(k, npr)
    xT = nc.dram_tensor("xT", [D, NT], F32, kind="ExternalInput").ap()
    aoT = nc.dram_tensor("aoT", [D, NT], BF16, kind="ExternalInput").ap()
    hout = nc.dram_tensor("hout", [D, CH], F32, kind="ExternalOutput").ap()
    wts = {}

    def din(name, shape):
        wts[name] = nc.dram_tensor(name, shape, F32, kind="ExternalInput").ap()
        return wts[name]

    lsem = P.dsem()
    for c in range(NKC):
        P.add("sp", lambda e, c=c: e.dma_start(out=k.h[:, c, :], in_=xT[c * 128:(c + 1) * 128, :]),
              writes=k.hb[c], dma=lsem)
    for c in range(NKC):
        for t in range(5):
            k.hb[c][t].w = (("dma", lsem), P.dma_cnt[lsem])
    asem = P.dsem()
    for c in range(NKC):
        P.add("sp", lambda e, c=c: e.dma_start(out=k.u[:, c, :], in_=aoT[c * 128:(c + 1) * 128, :]), writes=k.ub, dma=asem)
    k.next_gcol = cols["fn2b"]
    k.outproj(din("cout", [4, 128, 2048]), k.u, lambda kc, t: k.ub[t])
    k.next_gcol = cols["fn3a"]
    k.ffn(din("wg2b", [11, 128, 2048]), din("wu2b", [11, 128, 2048]), din("wd2b", [3, 4, 128, 2048]), cols["fn2b"])
    k.next_gcol = cols["mn3"]
    k.ffn(din("wg3a", [11, 128, 2048]), din("wu3a", [11, 128, 2048]), din("wd3a", [3, 4, 128, 2048]), cols["fn3a"])
    k.next_gcol = cols["fn3b"]
    k.mixer_a(din("ain1", [12, 128, 2048]), din("aout1", [4, 128, 2048]), cols["mn3"], cols["aconv1"])
    k.next_gcol = None
    k.ffn(din("wg3b", [11, 128, 2048]), din("wu3b", [11, 128, 2048]), din("wd3b", [3, 4, 128, 2048]), cols["fn3b"])
    osem = P.dsem()
    outb = Buf("out")
    for c in range(NKC):
        P.add("sp", lambda e, c=c: e.dma_start(out=hout[c * 128:(c + 1) * 128, :], in_=k.h[:, c, 16:16 + CH]),
              reads=[k.hb[c][t] for t in range(5)], writes=[outb], dma=osem)
    fin = k.es.enter_context(nc.sbuf_tensor("fin", [128, 1], F32))
    P.add("dve", lambda e: e.memset(fin[:, :], 0.0), reads=[outb])
    return k.finish()


def run_l3(inp, Hs, AO):
    pr = Params()
    cols = {}
    cols["eps"] = pr.add(np.full((128, 1), EPS, np.float32))
    cols["fn2b"] = pr.add(colvec(inp["ffn_norm"][2, 1]))
    cols["fn3a"] = pr.add(colvec(inp["ffn_norm"][3, 0]))
    cols["fn3b"] = pr.add(colvec(inp["ffn_norm"][3, 1]))
    cols["mn3"] = pr.add(colvec(inp["mix_norm"][3]))
    ac = inp["a_conv"][1]
    cols["aconv1"] = pr.add(np.concatenate([ac[:, j * 128:(j + 1) * 128].T for j in range(NKC)], axis=1))
    prm = pr.build()
    nc = build_l3(prm.shape[1], cols)
    W = {"cout": tile_in(inp["c_w_out"][0]),
         "wg2b": tile_in(inp["ffn_w_gate"][2, 1]), "wu2b": tile_in(inp["ffn_w_up"][2, 1]), "wd2b": tile_down(inp["ffn_w_down"][2, 1]),
         "wg3a": tile_in(inp["ffn_w_gate"][3, 0]), "wu3a": tile_in(inp["ffn_w_up"][3, 0]), "wd3a": tile_down(inp["ffn_w_down"][3, 0]),
         "wg3b": tile_in(inp["ffn_w_gate"][3, 1]), "wu3b": tile_in(inp["ffn_w_up"][3, 1]), "wd3b": tile_down(inp["ffn_w_down"][3, 1]),
         "ain1": tile_in(inp["a_w_in"][1]), "aout1": tile_in(inp["a_w_out"][1])}
    cst = consts_np()
    in_maps = []
    for c in range(NCORE):
        b, r = c // 4, c % 4
        s = CH * r
        m = {"xT": np.ascontiguousarray(Hs[b][s:s + NQ].T), "aoT": AO[b][r], "prm": prm, "cst": cst}
        m.update(W)
        in_maps.append(m)
    res = run_bass_kernel_spmd(nc, in_maps, core_ids=list(range(NCORE)))
    return res


def assemble(res1):
    Hs, Qs, Ks, Vs, LF = [], [], [], [], []
    for b in range(2):
        hp, qp, kp, vp, lp = [], [], [], [], []
        for r in range(4):
            o = res1.results[b * 4 + r]
            lo = 16 if r == 0 else 32
            hp.append(np.asarray(o["hout"])[:, lo:].T)
            qp.append(np.asarray(o["qT"])[:, lo:])
            kp.append(np.asarray(o["kT"])[:, lo:])
            vp.append(np.asarray(o["vo"])[lo:])
            lp.append(np.asarray(o["lfo"])[lo:])
        Hs.append(np.concatenate(hp, axis=0))
        Qs.append(np.concatenate(qp, axis=1))
        Ks.append(np.concatenate(kp, axis=1))
        Vs.append(np.concatenate(vp, axis=0))
        LF.append(np.concatenate(lp, axis=0))
    return Hs, Qs, Ks, Vs, LF


def kernel(**inputs):
    inp = prep_common(inputs)
    res1 = run_l1(inp, 8)
    Hs, Qs, Ks, Vs, LF = assemble(res1)
    AO = run_l2(Qs, Ks, Vs, LF)
    res3 = run_l3(inp, Hs, AO)
    out = np.empty((2, SEQ, D), np.float32)
    for c in range(NCORE):
        b, r = c // 4, c % 4
        out[b, r * CH:(r + 1) * CH] = np.asarray(res3.results[c]["hout"]).T
    return out
```

```python
import numpy as np
from contextlib import ExitStack
import concourse.bass as bass
import concourse.mybir as mybir
from concourse.bass_utils import run_bass_kernel_spmd

F32 = mybir.dt.float32
BF16 = mybir.dt.bfloat16
AF = mybir.ActivationFunctionType
ALU = mybir.AluOpType

D = 1024
DFF = 2816
NKC = 8
SEQ = 8192
NMETA = 16
NH = 16
HD = 64
EPS = 1e-6
NCORE = 8
CH = 2048
THIRDS = [(0, 8), (8, 8), (16, 6)]
NEG = -30000.0


class Buf:
    __slots__ = ("name", "w", "r")

    def __init__(self, name):
        self.name = name
        self.w = None
        self.r = {}


class Op:
    __slots__ = ("fn", "waits", "signal", "dma")

    def __init__(self, fn, dma):
        self.fn = fn
        self.waits = []
        self.signal = False
        self.dma = dma


ENGS = ["pe", "act", "dve", "pool", "sp"]


class Prog:
    def __init__(self):
        self.ops = {e: [] for e in ENGS}
        self.waited = {e: {} for e in ENGS}
        self.dma_cnt = []
        self.strict = False

    def dsem(self):
        self.dma_cnt.append(0)
        return len(self.dma_cnt) - 1

    def _need(self, eng, op, key, val):
        if key[0] == "eng" and key[1] == eng and not (self.strict and eng != "pe"):
            return
        if self.waited[eng].get(key, 0) >= val:
            return
        self.waited[eng][key] = val
        op.waits.append((key, val))
        if key[0] == "eng":
            self.ops[key[1]][val - 1].signal = True

    def add(self, eng, fn, reads=(), writes=(), dma=None):
        op = Op(fn, dma)
        for b in reads:
            if b.w is not None:
                self._need(eng, op, *b.w)
        for b in writes:
            if b.w is not None:
                self._need(eng, op, *b.w)
            for k, v in b.r.items():
                self._need(eng, op, k, v)
        self.ops[eng].append(op)
        if dma is not None:
            self.dma_cnt[dma] += 16
            ev = (("dma", dma), self.dma_cnt[dma])
        else:
            ev = (("eng", eng), len(self.ops[eng]))
        for b in reads:
            if b.r.get(ev[0], 0) < ev[1]:
                b.r[ev[0]] = ev[1]
        for b in writes:
            b.w = ev
            b.r = {}
        return op

    def emit(self, nc, es):
        esem = {e: es.enter_context(nc.semaphore("se_" + e)) for e in ENGS}
        dsem = [es.enter_context(nc.semaphore("sd_%d" % i)) for i in range(len(self.dma_cnt))]
        pref = {}
        for e in ENGS:
            c = 0
            arr = []
            for op in self.ops[e]:
                if op.signal and op.dma is None:
                    c += 1
                arr.append(c)
            pref[e] = arr
        block = es.enter_context(nc.Block())

        def run(name, eng):
            for op in self.ops[name]:
                for key, val in op.waits:
                    if key[0] == "eng":
                        eng.wait_ge(esem[key[1]], pref[key[1]][val - 1])
                    else:
                        eng.wait_ge(dsem[key[1]], val)
                inst = op.fn(eng)
                if op.dma is not None:
                    inst.then_inc(dsem[op.dma], 16)
                elif op.signal:
                    inst.then_inc(esem[name], 1)

        @block.tensor
        def _(e):
            run("pe", e)

        @block.scalar
        def _(e):
            run("act", e)

        @block.vector
        def _(e):
            run("dve", e)

        @block.gpsimd
        def _(e):
            run("pool", e)

        @block.sync
        def _(e):
            run("sp", e)


class KB:
    def __init__(self, pre):
        self.nc = bass.Bass("TRN2", target_bir_lowering=False)
        self.P = Prog()
        self.es = ExitStack()
        self.pre = pre
        self.NT = pre + CH
        self.tiles = [(0, pre)] + [(pre + 512 * i, 512) for i in range(4)]
        nc, es = self.nc, self.es
        NT = self.NT
        sb = lambda n, sh, dt: es.enter_context(nc.sbuf_tensor(n, sh, dt))
        self.h = sb("hT", [128, NKC, NT], F32)
        self.u = sb("uT", [128, NKC, NT], BF16)
        self.r1 = sb("r1", [128, NKC, NT], BF16)
        self.NW = 6
        self.w = [sb("w%d" % i, [128, 2048], BF16) for i in range(self.NW)]
        self.wb = [Buf("w%d" % i) for i in range(self.NW)]
        self.wsem = [self.P.dsem() for _ in range(self.NW)]
        self.wnext = 0
        self.sq = [sb("sq%d" % i, [128, 512], BF16) for i in range(2)]
        self.sqb = [Buf("sq%d" % i) for i in range(2)]
        self.sqn = 0
        self.sg = [sb("sg%d" % i, [128, 512], F32) for i in range(2)]
        self.sgb = [Buf("sg%d" % i) for i in range(2)]
        self.sgn = 0
        self.st = [sb("st%d" % i, [128, 512], F32) for i in range(3)]
        self.stb = [Buf("st%d" % i) for i in range(3)]
        self.ps = [es.enter_context(nc.psum_tensor("ps%d" % i, [128, 512], F32)) for i in range(8)]
        self.psb = [Buf("ps%d" % i) for i in range(8)]
        self.bankn = {"A": 0, "B": 0, "C": 0}
        self.hb = [[Buf("h%d_%d" % (c, t)) for t in range(5)] for c in range(NKC)]
        self.ub = [Buf("u%d" % t) for t in range(5)]
        self.r1b = [[Buf("r%d_%d" % (c, t)) for t in range(5)] for c in range(NKC)]
        self.ones = sb("ones", [128, 128], BF16)
        self.onesb = Buf("ones")
        self.ident = sb("ident", [128, 128], BF16)
        self.identb = Buf("ident")
        self.csem = self.P.dsem()
        self.next_gcol = None
        self.normed = None

    def hook(self, t):
        if self.next_gcol is not None:
            self.rmsnorm(t, self.next_gcol)

    def hooks_done(self):
        self.normed = self.next_gcol
        self.next_gcol = None

    def ensure_norm(self, gcol):
        if self.normed == gcol:
            self.normed = None
            return
        for t in range(len(self.tiles)):
            self.rmsnorm(t, gcol)

    def T(self, t):
        t0, n = self.tiles[t]
        self.P.strict = n < 256
        return t0, n

    def bank(self, grp):
        base = {"A": 0, "B": 2, "C": 4}[grp]
        i = base + (self.bankn[grp] % 2)
        self.bankn[grp] += 1
        return i

    def wslot(self):
        s = self.wnext % self.NW
        self.wnext += 1
        return s

    def load_w(self, dram_ap, L):
        s = self.wslot()
        wt = self.w[s]
        self.P.add("pool", lambda e, wt=wt, dram_ap=dram_ap, L=L: e.dma_start(out=wt[:, 0:L], in_=dram_ap),
                   writes=[self.wb[s]], dma=self.wsem[s])
        return s

    def load_const(self, sbuf_ap, dram_ap, buf, eng="pool"):
        self.P.add(eng, lambda e: e.dma_start(out=sbuf_ap, in_=dram_ap), writes=[buf], dma=self.csem)

    def rmsnorm(self, t, gcol):
        P = self.P
        t0, n = self.T(t)
        h, u, prm = self.h, self.u, self.prm
        msb, rsb = 6, 7
        for c in range(NKC):
            s = self.sqn % 2
            self.sqn += 1
            sq = self.sq[s]
            P.add("act", lambda e, sq=sq, c=c: e.activation(out=sq[:, 0:n], in_=h[:, c, t0:t0 + n], func=AF.Square),
                  reads=[self.hb[c][t]], writes=[self.sqb[s]])
            P.add("pe", lambda e, sq=sq, c=c: e.matmul(self.ps[msb][:, 0:n], lhsT=self.ones[:, :], rhs=sq[:, 0:n],
                                                       start=(c == 0), stop=(c == NKC - 1)),
                  reads=[self.sqb[s], self.onesb], writes=[self.psb[msb]])
        sd = self.st[0]
        P.add("act", lambda e: e.activation(out=sd[:, 0:n], in_=self.ps[msb][:, 0:n], func=AF.Sqrt,
                                            bias=prm[:, self.c_eps:self.c_eps + 1], scale=1.0 / D),
              reads=[self.psb[msb], self.prmb], writes=[self.stb[0]])
        P.add("dve", lambda e: e.reciprocal(out=self.ps[rsb][:, 0:n], in_=sd[:, 0:n]),
              reads=[self.stb[0]], writes=[self.psb[rsb]])
        for c in range(NKC):
            P.add("dve", lambda e, c=c: e.scalar_tensor_tensor(
                out=u[:, c, t0:t0 + n], in0=h[:, c, t0:t0 + n], scalar=prm[:, gcol + c:gcol + c + 1],
                in1=self.ps[rsb][:, 0:n], op0=ALU.mult, op1=ALU.mult),
                reads=[self.hb[c][t], self.psb[rsb], self.prmb], writes=[self.ub[t]])

    def ffn(self, wg, wu, wd, gcol):
        P = self.P
        h, u, r1 = self.h, self.u, self.r1
        nt = len(self.tiles)
        self.ensure_norm(gcol)
        for ti, (f0, nf) in enumerate(THIRDS):
            for fb in range(nf // 2):
                blk = f0 // 2 + fb
                sg_ = self.load_w(wg[blk], 2048)
                su_ = self.load_w(wu[blk], 2048)
                for fc in range(2):
                    fl = fb * 2 + fc
                    for t in range(nt):
                        t0, n = self.T(t)
                        ba, bb = self.bank("A"), self.bank("B")
                        for (bk, sl) in ((ba, sg_), (bb, su_)):
                            for kc in range(NKC):
                                P.add("pe", lambda e, bk=bk, sl=sl, kc=kc, fc=fc, t0=t0, n=n: e.matmul(
                                    self.ps[bk][:, 0:n], lhsT=self.w[sl][:, kc * 256 + fc * 128: kc * 256 + fc * 128 + 128],
                                    rhs=u[:, kc, t0:t0 + n], start=(kc == 0), stop=(kc == NKC - 1)),
                                    reads=[self.wb[sl], self.ub[t]], writes=[self.psb[bk]])
                        s = self.sgn % 2
                        self.sgn += 1
                        sgt = self.sg[s]
                        P.add("act", lambda e, sgt=sgt, ba=ba, n=n: e.activation(out=sgt[:, 0:n], in_=self.ps[ba][:, 0:n], func=AF.Silu),
                              reads=[self.psb[ba]], writes=[self.sgb[s]])
                        P.add("dve", lambda e, sgt=sgt, bb=bb, fl=fl, t0=t0, n=n: e.tensor_tensor(
                            out=r1[:, fl, t0:t0 + n], in0=sgt[:, 0:n], in1=self.ps[bb][:, 0:n], op=ALU.mult),
                            reads=[self.sgb[s], self.psb[bb]], writes=[self.r1b[fl][t]])
            if ti == len(THIRDS) - 1:
                L = nf * 256
                sds = [self.load_w(wd[ti, dblk][:, 0:L], L) for dblk in range(4)]
                prev_t = None
                for t in range(nt):
                    t0, n = self.T(t)
                    for d in range(NKC):
                        sd_, dc = sds[d // 2], d % 2
                        bc = self.bank("C")
                        for fl in range(nf):
                            P.add("pe", lambda e, bc=bc, sd_=sd_, fl=fl, dc=dc, t0=t0, n=n: e.matmul(
                                self.ps[bc][:, 0:n], lhsT=self.w[sd_][:, fl * 256 + dc * 128: fl * 256 + dc * 128 + 128],
                                rhs=r1[:, fl, t0:t0 + n], start=(fl == 0), stop=(fl == nf - 1)),
                                reads=[self.wb[sd_], self.r1b[fl][t]], writes=[self.psb[bc]])
                        P.add("dve", lambda e, bc=bc, d=d, t0=t0, n=n: e.scalar_tensor_tensor(
                            out=h[:, d, t0:t0 + n], in0=self.ps[bc][:, 0:n], scalar=0.5, in1=h[:, d, t0:t0 + n],
                            op0=ALU.mult, op1=ALU.add),
                            reads=[self.psb[bc], self.hb[d][t]], writes=[self.hb[d][t]])
                    if prev_t is not None:
                        self.hook(prev_t)
                    prev_t = t
                self.hook(prev_t)
                self.hooks_done()
                continue
            for dblk in range(4):
                L = nf * 256
                sd_ = self.load_w(wd[ti, dblk][:, 0:L], L)
                for dc in range(2):
                    d = dblk * 2 + dc
                    for t in range(nt):
                        t0, n = self.T(t)
                        bc = self.bank("C")
                        for fl in range(nf):
                            P.add("pe", lambda e, bc=bc, sd_=sd_, fl=fl, dc=dc, t0=t0, n=n: e.matmul(
                                self.ps[bc][:, 0:n], lhsT=self.w[sd_][:, fl * 256 + dc * 128: fl * 256 + dc * 128 + 128],
                                rhs=r1[:, fl, t0:t0 + n], start=(fl == 0), stop=(fl == nf - 1)),
                                reads=[self.wb[sd_], self.r1b[fl][t]], writes=[self.psb[bc]])
                        P.add("dve", lambda e, bc=bc, d=d, t0=t0, n=n: e.scalar_tensor_tensor(
                            out=h[:, d, t0:t0 + n], in0=self.ps[bc][:, 0:n], scalar=0.5, in1=h[:, d, t0:t0 + n],
                            op0=ALU.mult, op1=ALU.add),
                            reads=[self.psb[bc], self.hb[d][t]], writes=[self.hb[d][t]])

    def outproj(self, wo, src, srcb):
        P = self.P
        h = self.h
        nt = len(self.tiles)
        sos = [self.load_w(wo[dblk], 2048) for dblk in range(4)]
        prev_t = None
        for t in range(nt):
            t0, n = self.T(t)
            for d in range(NKC):
                so, dc = sos[d // 2], d % 2
                bc = self.bank("C")
                for kc in range(NKC):
                    P.add("pe", lambda e, bc=bc, so=so, kc=kc, dc=dc, t0=t0, n=n: e.matmul(
                        self.ps[bc][:, 0:n], lhsT=self.w[so][:, kc * 256 + dc * 128: kc * 256 + dc * 128 + 128],
                        rhs=src[:, kc, t0:t0 + n], start=(kc == 0), stop=(kc == NKC - 1)),
                        reads=[self.wb[so], srcb(kc, t)], writes=[self.psb[bc]])
                P.add("dve", lambda e, bc=bc, d=d, t0=t0, n=n: e.tensor_tensor(
                    out=h[:, d, t0:t0 + n], in0=self.ps[bc][:, 0:n], in1=h[:, d, t0:t0 + n], op=ALU.add),
                    reads=[self.psb[bc], self.hb[d][t]], writes=[self.hb[d][t]])
            if prev_t is not None:
                self.hook(prev_t)
            prev_t = t
        self.hook(prev_t)
        self.hooks_done()

    def proj(self, bk, sl, off, t):
        t0, n = self.T(t)
        for kc in range(NKC):
            self.P.add("pe", lambda e, kc=kc: e.matmul(
                self.ps[bk][:, 0:n], lhsT=self.w[sl][:, kc * 256 + off: kc * 256 + off + 128],
                rhs=self.u[:, kc, t0:t0 + n], start=(kc == 0), stop=(kc == NKC - 1)),
                reads=[self.wb[sl], self.ub[t]], writes=[self.psb[bk]])

    def mixer_a(self, win, wo, gcol, ccol):
        P = self.P
        prm, r1 = self.prm, self.r1
        nt = len(self.tiles)
        es, nc = self.es, self.nc
        if not hasattr(self, "cv"):
            self.cv = [es.enter_context(nc.sbuf_tensor("cv%d" % i, [128, 514], F32)) for i in range(2)]
            self.cvb = [Buf("cv%d" % i) for i in range(2)]
            self.acc = es.enter_context(nc.sbuf_tensor("cacc", [128, 512], F32))
            self.accb = Buf("cacc")
        self.ensure_norm(gcol)
        cvn = 0
        for jp in range(4):
            sb_ = self.load_w(win[jp], 2048)
            sc_ = self.load_w(win[4 + jp], 2048)
            sv_ = self.load_w(win[8 + jp], 2048)
            for jc in range(2):
                j = jp * 2 + jc
                wc = ccol + j * 3
                for t in range(nt):
                    t0, n = self.T(t)
                    ba, bb, bc = self.bank("A"), self.bank("B"), self.bank("C")
                    self.proj(ba, sb_, jc * 128, t)
                    self.proj(bb, sc_, jc * 128, t)
                    self.proj(bc, sv_, jc * 128, t)
                    cur, prv = self.cv[cvn % 2], self.cv[(cvn + 1) % 2]
                    curb, prvb = self.cvb[cvn % 2], self.cvb[(cvn + 1) % 2]
                    cvn += 1
                    s = self.sgn % 2
                    self.sgn += 1
                    sgt = self.sg[s]
                    P.add("act", lambda e, sgt=sgt, bb=bb, n=n: e.activation(out=sgt[:, 0:n], in_=self.ps[bb][:, 0:n], func=AF.Copy),
                          reads=[self.psb[bb]], writes=[self.sgb[s]])
                    if t == 0:
                        P.add("dve", lambda e, cur=cur: e.memset(cur[:, 0:2], 0.0), writes=[curb])
                    else:
                        pn = self.tiles[t - 1][1]
                        st_ = P.strict
                        P.strict = True
                        P.add("dve", lambda e, cur=cur, prv=prv, pn=pn: e.tensor_copy(out=cur[:, 0:2], in_=prv[:, pn:pn + 2]),
                              reads=[prvb], writes=[curb])
                        P.strict = st_
                    P.add("dve", lambda e, cur=cur, sgt=sgt, bc=bc, n=n: e.tensor_tensor(
                        out=cur[:, 2:2 + n], in0=sgt[:, 0:n], in1=self.ps[bc][:, 0:n], op=ALU.mult),
                        reads=[self.sgb[s], self.psb[bc]], writes=[curb])
                    acc = self.acc
                    P.add("dve", lambda e, cur=cur, n=n, wc=wc: e.tensor_scalar(
                        out=acc[:, 0:n], in0=cur[:, 2:2 + n], scalar1=prm[:, wc + 2:wc + 3], scalar2=None, op0=ALU.mult),
                        reads=[curb, self.prmb], writes=[self.accb])
                    P.add("dve", lambda e, cur=cur, n=n, wc=wc: e.scalar_tensor_tensor(
                        out=acc[:, 0:n], in0=cur[:, 1:1 + n], scalar=prm[:, wc + 1:wc + 2], in1=acc[:, 0:n],
                        op0=ALU.mult, op1=ALU.add), reads=[curb], writes=[self.accb])
                    P.add("dve", lambda e, cur=cur, n=n, wc=wc: e.scalar_tensor_tensor(
                        out=acc[:, 0:n], in0=cur[:, 0:n], scalar=prm[:, wc:wc + 1], in1=acc[:, 0:n],
                        op0=ALU.mult, op1=ALU.add), reads=[curb], writes=[self.accb])
                    P.add("dve", lambda e, n=n, ba=ba, j=j, t0=t0: e.tensor_tensor(
                        out=r1[:, j, t0:t0 + n], in0=acc[:, 0:n], in1=self.ps[ba][:, 0:n], op=ALU.mult),
                        reads=[self.accb, self.psb[ba]], writes=[self.r1b[j][t]])
        self.outproj(wo, r1, lambda kc, t: self.r1b[kc][t])

    def mixer_b(self, win, wo, gcol, ccol, bcol, lgcol, lbcol):
        P = self.P
        prm, r1, u = self.prm, self.r1, self.u
        nt = len(self.tiles)
        es, nc = self.es, self.nc
        self.gl = [es.enter_context(nc.sbuf_tensor("gl%d" % i, [128, 542], BF16)) for i in range(2)]
        self.glb = [Buf("gl%d" % i) for i in range(2)]
        self.dgs = [es.enter_context(nc.sbuf_tensor("dg%d" % i, [128, 31, 128], BF16)) for i in range(2)]
        self.dgbs = [Buf("dg%d" % i) for i in range(2)]
        self.zq = es.enter_context(nc.sbuf_tensor("zq", [128, 512], F32))
        self.zqb = Buf("zq")
        self.ensure_norm(gcol)
        gn = 0
        for jp in range(4):
            sa_ = self.load_w(win[jp], 2048)
            sg_ = self.load_w(win[4 + jp], 2048)
            for jc in range(2):
                j = jp * 2 + jc
                wc = ccol + j * 31
                dg, dgb = self.dgs[j % 2], self.dgbs[j % 2]
                P.strict = False
                for k in range(31):
                    P.add("dve", lambda e, k=k, wc=wc, dg=dg: e.tensor_scalar(
                        out=dg[:, k, :], in0=self.ident[:, :], scalar1=prm[:, wc + k:wc + k + 1], scalar2=None, op0=ALU.mult),
                        reads=[self.identb, self.prmb], writes=[dgb])
                for t in range(nt):
                    t0, n = self.T(t)
                    ba, bb, bc = self.bank("A"), self.bank("B"), self.bank("C")
                    self.proj(ba, sa_, jc * 128, t)
                    self.proj(bb, sg_, jc * 128, t)
                    cur, prv = self.gl[gn % 2], self.gl[(gn + 1) % 2]
                    curb, prvb = self.glb[gn % 2], self.glb[(gn + 1) % 2]
                    gn += 1
                    s = self.sgn % 2
                    self.sgn += 1
                    sgt = self.sg[s]
                    P.add("act", lambda e, sgt=sgt, bb=bb, n=n: e.activation(out=sgt[:, 0:n], in_=self.ps[bb][:, 0:n], func=AF.Sigmoid),
                          reads=[self.psb[bb]], writes=[self.sgb[s]])
                    if t == 0:
                        P.add("dve", lambda e, cur=cur: e.memset(cur[:, 0:30], 0.0), writes=[curb])
                    else:
                        pn = self.tiles[t - 1][1]
                        st_ = P.strict
                        P.strict = True
                        P.add("dve", lambda e, cur=cur, prv=prv, pn=pn: e.tensor_copy(out=cur[:, 0:30], in_=prv[:, pn:pn + 30]),
                              reads=[prvb], writes=[curb])
                        P.strict = st_
                    P.add("dve", lambda e, cur=cur, sgt=sgt, ba=ba, n=n: e.tensor_tensor(
                        out=cur[:, 30:30 + n], in0=sgt[:, 0:n], in1=self.ps[ba][:, 0:n], op=ALU.mult),
                        reads=[self.sgb[s], self.psb[ba]], writes=[curb])
                    for k in range(31):
                        P.add("pe", lambda e, k=k, bc=bc, cur=cur, n=n, dg=dg: e.matmul(
                            self.ps[bc][:, 0:n], lhsT=dg[:, k, :], rhs=cur[:, k:k + n], start=(k == 0), stop=(k == 30)),
                            reads=[dgb, curb], writes=[self.psb[bc]])
                    P.add("act", lambda e, bc=bc, j=j, t0=t0, n=n: e.activation(
                        out=r1[:, j, t0:t0 + n], in_=self.ps[bc][:, 0:n], func=AF.Identity,
                        bias=prm[:, bcol + j:bcol + j + 1], scale=1.0),
                        reads=[self.psb[bc], self.prmb], writes=[self.r1b[j][t]])
        for t in range(nt):
            t0, n = self.T(t)
            for c in range(NKC):
                P.add("pe", lambda e, c=c, t0=t0, n=n: e.matmul(self.ps[6][:, 0:n], lhsT=self.ones[:, :], rhs=r1[:, c, t0:t0 + n],
                                                                start=(c == 0), stop=(c == NKC - 1)),
                      reads=[self.r1b[c][t], self.onesb], writes=[self.psb[6]])
            for c in range(NKC):
                s = self.sqn % 2
                self.sqn += 1
                sq = self.sq[s]
                P.add("act", lambda e, sq=sq, c=c, t0=t0, n=n: e.activation(out=sq[:, 0:n], in_=r1[:, c, t0:t0 + n], func=AF.Square),
                      reads=[self.r1b[c][t]], writes=[self.sqb[s]])
                P.add("pe", lambda e, sq=sq, c=c, n=n: e.matmul(self.ps[7][:, 0:n], lhsT=self.ones[:, :], rhs=sq[:, 0:n],
                                                                start=(c == 0), stop=(c == NKC - 1)),
                      reads=[self.sqb[s], self.onesb], writes=[self.psb[7]])
            mean, var, sd = self.st[1], self.st[2], self.st[0]
            P.add("act", lambda e, n=n: e.activation(out=mean[:, 0:n], in_=self.ps[6][:, 0:n], func=AF.Copy, scale=1.0 / D),
                  reads=[self.psb[6]], writes=[self.stb[1]])
            P.add("dve", lambda e, n=n: e.tensor_tensor(out=var[:, 0:n], in0=mean[:, 0:n], in1=mean[:, 0:n], op=ALU.mult),
                  reads=[self.stb[1]], writes=[self.stb[2]])
            P.add("dve", lambda e, n=n: e.scalar_tensor_tensor(out=var[:, 0:n], in0=self.ps[7][:, 0:n], scalar=1.0 / D, in1=var[:, 0:n],
                                                               op0=ALU.mult, op1=ALU.subtract),
                  reads=[self.psb[7], self.stb[2]], writes=[self.stb[2]])
            P.add("act", lambda e, n=n: e.activation(out=sd[:, 0:n], in_=var[:, 0:n], func=AF.Sqrt,
                                                     bias=prm[:, self.c_eps:self.c_eps + 1], scale=1.0),
                  reads=[self.stb[2], self.prmb], writes=[self.stb[0]])
            P.add("dve", lambda e, n=n: e.reciprocal(out=self.ps[7][:, 0:n], in_=sd[:, 0:n]),
                  reads=[self.stb[0]], writes=[self.psb[7]])
            for c in range(NKC):
                zq = self.zq
                P.add("dve", lambda e, c=c, t0=t0, n=n: e.tensor_tensor(out=zq[:, 0:n], in0=r1[:, c, t0:t0 + n], in1=mean[:, 0:n], op=ALU.subtract),
                      reads=[self.r1b[c][t], self.stb[1]], writes=[self.zqb])
                P.add("dve", lambda e, c=c, n=n: e.scalar_tensor_tensor(out=zq[:, 0:n], in0=zq[:, 0:n], scalar=prm[:, lgcol + c:lgcol + c + 1],
                                                                   in1=self.ps[7][:, 0:n], op0=ALU.mult, op1=ALU.mult),
                      reads=[self.psb[7], self.prmb], writes=[self.zqb])
                P.add("act", lambda e, c=c, t0=t0, n=n: e.activation(out=u[:, c, t0:t0 + n], in_=zq[:, 0:n], func=AF.Silu,
                                                                bias=prm[:, lbcol + c:lbcol + c + 1], scale=1.0),
                      reads=[self.zqb, self.prmb], writes=[self.ub[t]])
        self.outproj(wo, u, lambda kc, t: self.ub[t])

    def finish(self):
        self.P.emit(self.nc, self.es)
        self.es.close()
        return self.nc


def tile_in(w, n=256):
    K, N = w.shape
    kc = K // 128
    return np.ascontiguousarray(w.reshape(kc, 128, N // n, n).transpose(2, 1, 0, 3).reshape(N // n, 128, kc * n))


def tile_down(wd):
    out = np.zeros((3, 4, 128, 2048), np.float32)
    for ti, (f0, nf) in enumerate(THIRDS):
        blk = wd[f0 * 128:(f0 + nf) * 128]
        out[ti, :, :, :nf * 256] = blk.reshape(nf, 128, 4, 256).transpose(2, 1, 0, 3).reshape(4, 128, nf * 256)
    return out


def colvec(v):
    return np.ascontiguousarray(v.reshape(NKC, 128).T)


class Params:
    def __init__(self):
        self.cols = []
        self.n = 0

    def add(self, arr):
        o = self.n
        self.cols.append(np.asarray(arr, np.float32))
        self.n += arr.shape[1]
        return o

    def build(self):
        return np.ascontiguousarray(np.concatenate(self.cols, axis=1))


def setup_consts(k, npr):
    nc, P = k.nc, k.P
    k.prm_d = nc.dram_tensor("prm", [128, npr], F32, kind="ExternalInput").ap()
    k.prm = k.es.enter_context(nc.sbuf_tensor("prm_sb", [128, npr], F32))
    k.prmb = Buf("prm")
    k.prmsem = P.dsem()
    P.add("sp", lambda e: e.dma_start(out=k.prm[:, :], in_=k.prm_d[:, :]), writes=[k.prmb], dma=k.prmsem)
    k.cst_d = nc.dram_tensor("cst", [128, 384], F32, kind="ExternalInput").ap()
    s1, s2, s3 = P.dsem(), P.dsem(), P.dsem()
    k.bones = k.es.enter_context(nc.sbuf_tensor("bones", [128, 128], BF16))
    k.bonesb = Buf("bones")
    P.add("pool", lambda e: e.dma_start(out=k.ones[:, :], in_=k.cst_d[:, 0:128]), writes=[k.onesb], dma=s1)
    P.add("pool", lambda e: e.dma_start(out=k.ident[:, :], in_=k.cst_d[:, 128:256]), writes=[k.identb], dma=s2)
    P.add("pool", lambda e: e.dma_start(out=k.bones[:, :], in_=k.cst_d[:, 256:384]), writes=[k.bonesb], dma=s3)


def consts_np():
    bo = np.zeros((128, 128), np.float32)
    bo[:64, :64] = 1.0
    bo[64:, 64:] = 1.0
    return np.ascontiguousarray(np.concatenate([np.ones((128, 128), np.float32), np.eye(128, dtype=np.float32), bo], axis=1))


def build_l1(stages, npr, cols):
    k = KB(pre=32)
    nc, P = k.nc, k.P
    NT = k.NT
    k.c_eps = cols["eps"]
    setup_consts(k, npr)
    xT = nc.dram_tensor("xT", [D, NT], F32, kind="ExternalInput").ap()
    hout = nc.dram_tensor("hout", [D, NT], F32, kind="ExternalOutput").ap()
    wts = {}

    def din(name, shape):
        wts[name] = nc.dram_tensor(name, shape, F32, kind="ExternalInput").ap()
        return wts[name]

    lsem = P.dsem()
    for c in range(NKC):
        P.add("sp", lambda e, c=c: e.dma_start(out=k.h[:, c, :], in_=xT[c * 128:(c + 1) * 128, :]),
              writes=k.hb[c], dma=lsem)
    for c in range(NKC):
        for t in range(5):
            k.hb[c][t].w = (("dma", lsem), P.dma_cnt[lsem])
    seq = []
    for L in range(3):
        seq.append(("ffa", L, cols["fn%da" % L]))
        seq.append(("mix", L, cols["mn%d" % L]))
        if L < 2:
            seq.append(("ffb", L, cols["fn%db" % L]))
    seq = seq[:stages]
    for i, (kind, L, gc) in enumerate(seq):
        k.next_gcol = seq[i + 1][2] if i + 1 < len(seq) else None
        if kind == "ffa":
            k.ffn(din("wg%da" % L, [11, 128, 2048]), din("wu%da" % L, [11, 128, 2048]), din("wd%da" % L, [3, 4, 128, 2048]), gc)
        elif kind == "ffb":
            k.ffn(din("wg%db" % L, [11, 128, 2048]), din("wu%db" % L, [11, 128, 2048]), din("wd%db" % L, [3, 4, 128, 2048]), gc)
        elif L == 0:
            k.mixer_a(din("ain0", [12, 128, 2048]), din("aout0", [4, 128, 2048]), gc, cols["aconv0"])
        elif L == 1:
            k.mixer_b(din("bin", [8, 128, 2048]), din("bout", [4, 128, 2048]), gc, cols["bconv"], cols["bcb"],
                      cols["blg"], cols["blb"])
        else:
            k.next_gcol = None
            attn_proj(k, din, cols)
    osem = P.dsem()
    outb = Buf("out")
    for c in range(NKC):
        P.add("sp", lambda e, c=c: e.dma_start(out=hout[c * 128:(c + 1) * 128, :], in_=k.h[:, c, :]),
              reads=[k.hb[c][t] for t in range(5)], writes=[outb], dma=osem)
    fin = k.es.enter_context(nc.sbuf_tensor("fin", [128, 1], F32))
    P.add("dve", lambda e: e.memset(fin[:, :], 0.0), reads=[outb], writes=getattr(k, "extra_out", []))
    return k.finish(), list(wts.keys())


def attn_proj(k, din, cols):
    nc, P, es = k.nc, k.P, k.es
    NT = k.NT
    prm, u = k.prm, k.u
    nt = len(k.tiles)
    cin = din("cin", [12, 128, 2048])
    wf_d = din("cwf", [128, 128])
    bfb_d = din("bfb", [128, 64])
    qT = nc.dram_tensor("qT", [D, NT], BF16, kind="ExternalOutput").ap()
    kT = nc.dram_tensor("kT", [D, NT], BF16, kind="ExternalOutput").ap()
    vo = nc.dram_tensor("vo", [NT, D], BF16, kind="ExternalOutput").ap()
    lfo = nc.dram_tensor("lfo", [NT, NH], F32, kind="ExternalOutput").ap()
    og = [es.enter_context(nc.sbuf_tensor("og%d" % i, [128, 512], BF16)) for i in range(2)]
    ogb = [Buf("og%d" % i) for i in range(2)]
    ogs = [P.dsem() for _ in range(2)]
    lst = [es.enter_context(nc.sbuf_tensor("lst%d" % i, [128, 64], F32)) for i in range(2)]
    lstb = [Buf("lst%d" % i) for i in range(2)]
    lss = [P.dsem() for _ in range(2)]
    bfb = es.enter_context(nc.sbuf_tensor("bfb_sb", [128, 64], F32))
    bfbb = Buf("bfb")
    P.add("sp", lambda e: e.dma_start(out=bfb[:, :], in_=bfb_d[:, :]), writes=[bfbb], dma=P.dsem())
    outs = []
    k.ensure_norm(cols["mn2"])
    ogn = 0
    for which, dst, gcol, scl, bcol in (("q", qT, cols["qg"], 1.0, cols["eps64"]), ("k", kT, cols["kg"], 1.0 / HD, cols["eps"])):
        base = 0 if which == "q" else 4
        for jp in range(4):
            sl = k.load_w(cin[base + jp], 2048)
            for jc in range(2):
                j = jp * 2 + jc
                for t in range(nt):
                    t0, n = k.T(t)
                    ba = k.bank("A")
                    k.proj(ba, sl, jc * 128, t)
                    s = k.sqn % 2
                    k.sqn += 1
                    sq = k.sq[s]
                    P.add("act", lambda e, sq=sq, ba=ba, n=n: e.activation(out=sq[:, 0:n], in_=k.ps[ba][:, 0:n], func=AF.Square),
                          reads=[k.psb[ba]], writes=[k.sqb[s]])
                    P.add("pe", lambda e, sq=sq, n=n: e.matmul(k.ps[6][:, 0:n], lhsT=k.bones[:, :], rhs=sq[:, 0:n], start=True, stop=True),
                          reads=[k.sqb[s], k.bonesb], writes=[k.psb[6]])
                    P.add("act", lambda e, n=n, scl=scl, bcol=bcol: e.activation(out=k.st[0][:, 0:n], in_=k.ps[6][:, 0:n], func=AF.Sqrt,
                                                                       bias=prm[:, bcol:bcol + 1], scale=scl),
                          reads=[k.psb[6], k.prmb], writes=[k.stb[0]])
                    P.add("dve", lambda e, n=n: e.reciprocal(out=k.st[1][:, 0:n], in_=k.st[0][:, 0:n]),
                          reads=[k.stb[0]], writes=[k.stb[1]])
                    o = ogn % 2
                    ogn += 1
                    P.add("dve", lambda e, o=o, ba=ba, n=n, gcol=gcol: e.scalar_tensor_tensor(
                        out=og[o][:, 0:n], in0=k.ps[ba][:, 0:n], scalar=prm[:, gcol:gcol + 1], in1=k.st[1][:, 0:n],
                        op0=ALU.mult, op1=ALU.mult), reads=[k.psb[ba], k.stb[1], k.prmb], writes=[ogb[o]])
                    P.add("sp", lambda e, o=o, dst=dst, j=j, t0=t0, n=n: e.dma_start(out=dst[j * 128:(j + 1) * 128, t0:t0 + n], in_=og[o][:, 0:n]),
                          reads=[ogb[o]], dma=ogs[o])
    for vb in range(4):
        sl = k.load_w(cin[8 + vb], 2048)
        for t in range(nt):
            t0, n = k.T(t)
            subs = [(0, n)] if n <= 128 else [(a * 128, 128) for a in range(n // 128)]
            for g0 in range(0, len(subs), 2):
                grp = subs[g0:g0 + 2]
                bb = k.bank("B")
                for gi, (so, m) in enumerate(grp):
                    for kc in range(NKC):
                        P.add("pe", lambda e, bb=bb, gi=gi, so=so, m=m, kc=kc, sl=sl, t0=t0: e.matmul(
                            k.ps[bb][0:m, gi * 256:(gi + 1) * 256], lhsT=u[:, kc, t0 + so:t0 + so + m],
                            rhs=k.w[sl][:, kc * 256:(kc + 1) * 256], start=(kc == 0), stop=(kc == NKC - 1)),
                            reads=[k.wb[sl], k.ub[t]], writes=[k.psb[bb]])
                m = grp[0][1]
                ng = len(grp)
                o = ogn % 2
                ogn += 1
                P.add("act", lambda e, o=o, bb=bb, m=m, ng=ng: e.activation(out=og[o][0:m, 0:ng * 256], in_=k.ps[bb][0:m, 0:ng * 256], func=AF.Copy),
                      reads=[k.psb[bb]], writes=[ogb[o]])
                r0 = t0 + grp[0][0]
                if ng == 2:
                    P.add("sp", lambda e, o=o, r0=r0, vb=vb: e.dma_start(
                        out=vo[r0:r0 + 256, vb * 256:(vb + 1) * 256].rearrange("(g p) c -> p g c", p=128),
                        in_=og[o][:, 0:512].rearrange("p (g c) -> p g c", g=2)), reads=[ogb[o]], dma=ogs[o])
                else:
                    P.add("sp", lambda e, o=o, r0=r0, vb=vb, m=m: e.dma_start(
                        out=vo[r0:r0 + m, vb * 256:(vb + 1) * 256], in_=og[o][0:m, 0:256]), reads=[ogb[o]], dma=ogs[o])
    wfs = k.wslot()
    P.add("pool", lambda e: e.dma_start(out=k.w[wfs][:, 0:128], in_=wf_d[:, :]), writes=[k.wb[wfs]], dma=k.wsem[wfs])
    ln = 0
    for t in range(nt):
        t0, n = k.T(t)
        subs = [(0, n)] if n <= 128 else [(a * 128, 128) for a in range(n // 128)]
        bb = k.bank("B")
        for gi, (so, m) in enumerate(subs):
            for kc in range(NKC):
                P.add("pe", lambda e, bb=bb, gi=gi, so=so, m=m, kc=kc, t0=t0: e.matmul(
                    k.ps[bb][0:m, gi * 16:(gi + 1) * 16], lhsT=u[:, kc, t0 + so:t0 + so + m],
                    rhs=k.w[wfs][:, kc * 16:(kc + 1) * 16], start=(kc == 0), stop=(kc == NKC - 1)),
                    reads=[k.wb[wfs], k.ub[t]], writes=[k.psb[bb]])
        m = subs[0][1]
        ng = len(subs)
        o = ln % 2
        ln += 1
        W_ = ng * 16
        P.strict = True
        P.add("dve", lambda e, o=o, bb=bb, m=m, W_=W_: e.tensor_tensor(out=lst[o][0:m, 0:W_], in0=k.ps[bb][0:m, 0:W_], in1=bfb[0:m, 0:W_], op=ALU.add),
              reads=[k.psb[bb], bfbb], writes=[lstb[o]])
        P.add("act", lambda e, o=o, m=m, W_=W_: e.activation(out=lst[o][0:m, 0:W_], in_=lst[o][0:m, 0:W_], func=AF.Exp, scale=-1.0),
              reads=[lstb[o]], writes=[lstb[o]])
        P.add("act", lambda e, o=o, m=m, W_=W_: e.activation(out=lst[o][0:m, 0:W_], in_=lst[o][0:m, 0:W_], func=AF.Ln,
                                                        bias=prm[0:m, cols["one"]:cols["one"] + 1], scale=1.0),
              reads=[lstb[o], k.prmb], writes=[lstb[o]])
        P.add("dve", lambda e, o=o, m=m, W_=W_: e.tensor_scalar(out=lst[o][0:m, 0:W_], in0=lst[o][0:m, 0:W_], scalar1=-1.0, scalar2=None, op0=ALU.mult),
              reads=[lstb[o]], writes=[lstb[o]])
        if ng > 1:
            P.add("sp", lambda e, o=o, t0=t0, n=n, ng=ng: e.dma_start(
                out=lfo[t0:t0 + n, :].rearrange("(g p) c -> p g c", p=128),
                in_=lst[o][:, 0:ng * 16].rearrange("p (g c) -> p g c", g=ng)), reads=[lstb[o]], dma=lss[o])
        else:
            P.add("sp", lambda e, o=o, t0=t0, m=m: e.dma_start(out=lfo[t0:t0 + m, :], in_=lst[o][0:m, 0:16]), reads=[lstb[o]], dma=lss[o])
    k.extra_out = ogb + lstb


def prep_common(inputs):
    f = lambda a: np.asarray(a, np.float32)
    return {kk: f(v) for kk, v in inputs.items()}


def run_l1(inp, stages):
    pr = Params()
    cols = {}
    cols["eps"] = pr.add(np.full((128, 1), EPS, np.float32))
    cols["eps64"] = pr.add(np.full((128, 1), HD * EPS, np.float32))
    cols["one"] = pr.add(np.ones((128, 1), np.float32))
    for L in range(3):
        cols["fn%da" % L] = pr.add(colvec(inp["ffn_norm"][L, 0]))
        cols["fn%db" % L] = pr.add(colvec(inp["ffn_norm"][L, 1]))
        cols["mn%d" % L] = pr.add(colvec(inp["mix_norm"][L]))
    ac = inp["a_conv"][0]
    cols["aconv0"] = pr.add(np.concatenate([ac[:, j * 128:(j + 1) * 128].T for j in range(NKC)], axis=1))
    bc = inp["b_conv"][0]
    cols["bconv"] = pr.add(np.concatenate([bc[:, j * 128:(j + 1) * 128].T for j in range(NKC)], axis=1))
    cols["bcb"] = pr.add(colvec(inp["b_conv_bias"][0]))
    cols["blg"] = pr.add(colvec(inp["b_ln_g"][0]))
    cols["blb"] = pr.add(colvec(inp["b_ln_b"][0]))
    cols["qg"] = pr.add(np.tile(inp["c_q_norm"][0], 2)[:, None])
    cols["kg"] = pr.add(np.tile(inp["c_k_norm"][0], 2)[:, None])
    prm = pr.build()
    nc, wnames = build_l1(stages, prm.shape[1], cols)
    W = {}
    for L in range(3):
        for s, si in (("a", 0), ("b", 1)):
            if "wg%d%s" % (L, s) in wnames:
                W["wg%d%s" % (L, s)] = tile_in(inp["ffn_w_gate"][L, si])
                W["wu%d%s" % (L, s)] = tile_in(inp["ffn_w_up"][L, si])
                W["wd%d%s" % (L, s)] = tile_down(inp["ffn_w_down"][L, si])
    if "ain0" in wnames:
        W["ain0"] = tile_in(inp["a_w_in"][0])
        W["aout0"] = tile_in(inp["a_w_out"][0])
    if "bin" in wnames:
        W["bin"] = tile_in(inp["b_w_in"][0])
        W["bout"] = tile_in(inp["b_w_out"][0])
    if "cin" in wnames:
        cw = inp["c_w_in"][0]
        W["cin"] = tile_in(np.ascontiguousarray(cw[:, :3 * D]))
        W["cwf"] = np.ascontiguousarray(cw[:, 3 * D:].reshape(NKC, 128, NH).transpose(1, 0, 2).reshape(128, NKC * NH))
        W["bfb"] = np.ascontiguousarray(np.broadcast_to(np.tile(inp["c_b_f"][0], 4)[None, :], (128, 64)))
    cst = consts_np()
    x, meta = inp["x"], inp["meta"]
    in_maps = []
    for c in range(NCORE):
        b, r = c // 4, c % 4
        if r == 0:
            tok = np.concatenate([np.zeros((16, D), np.float32), meta, x[b, 0:CH]], axis=0)
        else:
            tok = x[b, r * CH - 32:(r + 1) * CH]
        m = {"xT": np.ascontiguousarray(tok.T), "prm": prm, "cst": cst}
        m.update(W)
        in_maps.append(m)
    res = run_bass_kernel_spmd(nc, in_maps, core_ids=list(range(NCORE)))
    return res


NQ = CH + 16


def job_dims(g):
    hc = 3 - g
    return hc, hc * CH + NQ, 16 * hc + 17


def build_l2():
    nc = bass.Bass("TRN2", target_bir_lowering=False)
    P = Prog()
    es = ExitStack()
    sb = lambda n, sh, dt: es.enter_context(nc.sbuf_tensor(n, sh, dt))
    NKM, NKBM = job_dims(0)[1], job_dims(0)[2]
    Ka = [sb("Ka%d" % i, [128, NKM], BF16) for i in range(2)]
    Va = [sb("Va%d" % i, [128, NKBM * 128], BF16) for i in range(2)]
    Qa = [sb("Qa%d" % i, [128, NQ], BF16) for i in range(2)]
    Kb = [Buf("Ka%d" % i) for i in range(2)]
    Vb = [Buf("Va%d" % i) for i in range(2)]
    Qb = [Buf("Qa%d" % i) for i in range(2)]
    Ks = [P.dsem() for _ in range(2)]
    Vs = [P.dsem() for _ in range(2)]
    Qs = [P.dsem() for _ in range(2)]
    Gs = [P.dsem() for _ in range(2)]
    lfT = sb("lfT", [4, NKM], F32)
    lfb = Buf("lfT")
    zer = sb("zer", [4, NKM], F32)
    zerb = Buf("zer")
    cumT = sb("cumT", [4, NKM], F32)
    cumb = Buf("cumT")
    gq = sb("gq", [4, NQ], BF16)
    gqb = Buf("gq")
    nb = sb("nb", [128, NKBM * 4], F32)
    nbb = Buf("nb")
    PT = [sb("PT%d" % i, [128, 512], BF16) for i in range(3)]
    PTb = [Buf("PT%d" % i) for i in range(3)]
    tri = sb("tri", [128, 2048], BF16)
    trib = Buf("tri")
    identb_ = sb("identb", [128, 128], BF16)
    identbb = Buf("identb")
    identf = sb("identf", [128, 128], F32)
    identfb = Buf("identf")
    onesf = sb("onesf", [128, 128], F32)
    onesfb = Buf("onesf")
    rc = sb("rc", [128, 512], F32)
    rcb = Buf("rc")
    rbt = sb("rbt", [128, 512], F32)
    rbb = Buf("rbt")
    ost = [sb("ost%d" % i, [128, 512], BF16) for i in range(2)]
    ostb = [Buf("ost%d" % i) for i in range(2)]
    osts = [P.dsem() for _ in range(2)]
    ps = [es.enter_context(nc.psum_tensor("ps%d" % i, [128, 512], F32)) for i in range(8)]
    psb = [Buf("ps%d" % i) for i in range(8)]
    cst = nc.dram_tensor("cst2", [128, 2304], F32, kind="ExternalInput").ap()
    P.add("pool", lambda e: e.dma_start(out=tri[:, :], in_=cst[:, 0:2048]), writes=[trib], dma=P.dsem())
    P.add("pool", lambda e: e.dma_start(out=identb_[:, :], in_=cst[:, 2048:2176]), writes=[identbb], dma=P.dsem())
    P.add("sp", lambda e: e.dma_start(out=identf[:, :], in_=cst[:, 2048:2176]), writes=[identfb], dma=P.dsem())
    P.add("sp", lambda e: e.dma_start(out=onesf[:, :], in_=cst[:, 2176:2304]), writes=[onesfb], dma=P.dsem())
    P.add("pool", lambda e: e.memset(zer[:, :], 0.0), writes=[zerb])
    for i in range(2):
        P.add("dve", lambda e, i=i: e.memset(Ka[i][64:65, :], 1.0), writes=[Kb[i]])
    P.add("pool", lambda e: e.memset(Va[0][:, :].rearrange("p (k c) -> p k c", c=128)[:, :, 64:128], 1.0), writes=[Vb[0]])
    P.add("pool", lambda e: e.memset(Va[1][:, :].rearrange("p (k c) -> p k c", c=128)[:, :, 0:64], 1.0), writes=[Vb[1]])
    qtiles = [(0, 16)] + [(16 + 512 * i, 512) for i in range(4)]
    hn = 0
    sbn = 0
    ptn = 0
    accn = 0
    osn = 0
    outs = []
    lsem = P.dsem()
    jobs = []
    lds = {}
    for g in range(4):
        hc, NK, NKB = job_dims(g)
        Hb = 16 * hc
        qd = nc.dram_tensor("q%d" % g, [4, HD, NQ], BF16, kind="ExternalInput").ap()
        kd = nc.dram_tensor("k%d" % g, [4, HD, NK], BF16, kind="ExternalInput").ap()
        vd = nc.dram_tensor("v%d" % g, [4, 128, NKB * HD], BF16, kind="ExternalInput").ap()
        ld = nc.dram_tensor("l%d" % g, [4, NK], F32, kind="ExternalInput").ap()
        ao = nc.dram_tensor("ao%d" % g, [4 * HD, NQ], BF16, kind="ExternalOutput").ap()
        aob = Buf("ao%d" % g)
        outs.append(aob)
        kbt = [(kb * 128, 128) for kb in range(Hb)] + [(Hb * 128, 16)] + [(Hb * 128 + 16 + b * 128, 128) for b in range(16)]
        lds[g] = ld
        jobs.append((g, hc, NK, NKB, Hb, qd, kd, vd, ao, aob, kbt))
    heads = []
    for (g, hc, NK, NKB, Hb, qd, kd, vd, ao, aob, kbt) in jobs:
        for hl in range(4):
            heads.append((g, hl))
    jobd = {j[0]: j for j in jobs}

    def emit_loads(hi):
        g, hl = heads[hi]
        (_, hc, NK, NKB, Hb, qd, kd, vd, ao, aob, kbt) = jobd[g]
        sl = hi % 2
        P.add("sp", lambda e: e.dma_start(out=Qa[sl][0:64, :], in_=qd[hl]), writes=[Qb[sl]], dma=Qs[sl])
        P.add("sp", lambda e: e.dma_start(out=Ka[sl][0:64, 0:NK], in_=kd[hl]), writes=[Kb[sl]], dma=Ks[sl])
        vc0 = 0 if sl == 0 else 64
        hk = NKB // 2
        for (ka, kz) in ((0, hk), (hk, NKB)):
            P.add("sp", lambda e, ka=ka, kz=kz: e.dma_start(
                out=Va[sl][:, 0:NKB * 128].rearrange("p (k c) -> p k c", c=128)[:, ka:kz, vc0:vc0 + 64],
                in_=vd[hl].rearrange("p (k c) -> p k c", c=64)[:, ka:kz, :]), writes=[Vb[sl]], dma=Vs[sl])

    def emit_jobsetup(g):
        (_, hc, NK, NKB, Hb, qd, kd, vd, ao, aob, kbt) = jobd[g]
        ld = lds[g]
        P.add("sp", lambda e: e.dma_start(out=lfT[:, 0:NK], in_=ld[:, :]), writes=[lfb], dma=lsem)
        P.add("dve", lambda e: e.tensor_tensor_scan(out=cumT[:, 0:NK], data0=lfT[:, 0:NK], data1=zer[:, 0:NK], initial=0.0,
                                                    op0=ALU.add, op1=ALU.add),
              reads=[lfb, zerb], writes=[cumb])
        P.add("dve", lambda e: e.tensor_copy(out=gq[:, :], in_=cumT[:, NK - NQ:NK]), reads=[cumb], writes=[gqb])
        for kb, (k0, m) in enumerate(kbt):
            P.add("pe", lambda e, kb=kb, k0=k0, m=m: e.transpose(out=ps[6][0:m, kb * 4:kb * 4 + 4], in_=cumT[0:4, k0:k0 + m],
                                                              identity=identf[0:4, 0:4]),
                  reads=[cumb, identfb], writes=[psb[6]])
        P.add("dve", lambda e: e.tensor_scalar(out=nb[:, 0:NKB * 4], in0=ps[6][:, 0:NKB * 4], scalar1=-1.0, scalar2=None, op0=ALU.mult),
              reads=[psb[6]], writes=[nbb])

    units = []
    for hi, (g, hl) in enumerate(heads):
        (_, hc, NK, NKB, Hb, qd, kd, vd, ao, aob, kbt) = jobd[g]
        for ti, (t0, n) in enumerate(qtiles):
            blocks = [(kb, None) for kb in range(Hb)]
            if ti == 0:
                blocks.append((Hb, 0))
            else:
                blocks.append((Hb, None))
                for b in range(4 * (ti - 1)):
                    blocks.append((Hb + 1 + b, None))
                for b in range(4 * (ti - 1), 4 * ti):
                    blocks.append((Hb + 1 + b, (b - 4 * (ti - 1)) * 128))
            for bi, (kb, o) in enumerate(blocks):
                units.append(dict(hi=hi, g=g, hl=hl, ti=ti, t0=t0, n=n, kb=kb, o=o, bi=bi, nbk=len(blocks),
                                  first_of_head=(ti == 0 and bi == 0), first_of_job=(ti == 0 and bi == 0 and hl == 0)))
    state = dict(sbn=0, ptn=0, accn=0, osn=0)
    pending = []

    def emit_qk(un):
        hi, g, hl = un["hi"], un["g"], un["hl"]
        if un["first_of_job"]:
            emit_jobsetup(g)
        if un["first_of_head"]:
            if hi == 0:
                emit_loads(0)
            sl = hi % 2
            P.add("sp", lambda e: e.dma_start(out=Qa[sl][64:65, :], in_=gq[hl:hl + 1, :]), reads=[gqb], writes=[Qb[sl]], dma=Qs[sl])
            if hi + 1 < len(heads):
                emit_loads(hi + 1)
        (_, hc, NK, NKB, Hb, qd, kd, vd, ao, aob, kbt) = jobd[g]
        sl = hi % 2
        k0, m = kbt[un["kb"]]
        t0, n, o = un["t0"], un["n"], un["o"]
        if un["bi"] == 0:
            un["acc"] = 4 + (state["accn"] % 2)
            state["accn"] += 1
            state["curacc"] = un["acc"]
        else:
            un["acc"] = state["curacc"]
        sbk = state["sbn"] % 4
        state["sbn"] += 1
        un["sbk"] = sbk
        P.add("pe", lambda e: e.matmul(ps[sbk][0:m, 0:n], lhsT=Ka[sl][0:65, k0:k0 + m], rhs=Qa[sl][0:65, t0:t0 + n],
                                       start=True, stop=(o is None)),
              reads=[Kb[sl], Qb[sl]], writes=[psb[sbk]])
        if o is not None:
            oo = (o // 128) * 512
            P.add("pe", lambda e: e.matmul(ps[sbk][0:m, 0:n], lhsT=identb_[0:m, 0:m], rhs=tri[0:m, oo:oo + n], start=False, stop=True),
                  reads=[identbb, trib], writes=[psb[sbk]])

    def emit_pv(un):
        hi, g, hl = un["hi"], un["g"], un["hl"]
        (_, hc, NK, NKB, Hb, qd, kd, vd, ao, aob, kbt) = jobd[g]
        sl = hi % 2
        kb = un["kb"]
        k0, m = kbt[kb]
        t0, n = un["t0"], un["n"]
        sbk, acc = un["sbk"], un["acc"]
        pi = state["ptn"] % 3
        state["ptn"] += 1
        P.add("act", lambda e: e.activation(out=PT[pi][0:m, 0:n], in_=ps[sbk][0:m, 0:n], func=AF.Exp,
                                            bias=nb[0:m, kb * 4 + hl:kb * 4 + hl + 1], scale=1.0),
              reads=[psb[sbk], nbb], writes=[PTb[pi]])
        P.add("pe", lambda e: e.matmul(ps[acc][:, 0:n], lhsT=Va[sl][0:m, kb * 128:(kb + 1) * 128], rhs=PT[pi][0:m, 0:n],
                                       start=(un["bi"] == 0), stop=(un["bi"] == un["nbk"] - 1)),
              reads=[Vb[sl], PTb[pi]], writes=[psb[acc]])
        if un["bi"] == un["nbk"] - 1:
            even = (sl == 0)
            p0 = 64 if even else 0
            while pending:
                emit_fin(pending.pop(0)[1])
            P.add("dve", lambda e: e.reciprocal(out=rc[p0:p0 + 1, 0:n], in_=ps[acc][p0:p0 + 1, 0:n]),
                  reads=[psb[acc]], writes=[rcb])
            pending.append([4, un])

    def emit_fin(un):
        hi, g, hl = un["hi"], un["g"], un["hl"]
        (_, hc, NK, NKB, Hb, qd, kd, vd, ao, aob, kbt) = jobd[g]
        sl = hi % 2
        even = (sl == 0)
        p0 = 64 if even else 0
        q0 = 0 if even else 64
        t0, n, acc = un["t0"], un["n"], un["acc"]
        P.add("pe", lambda e: e.matmul(ps[7][:, 0:n], lhsT=onesf[p0:p0 + 1, 0:128], rhs=rc[p0:p0 + 1, 0:n], start=True, stop=True),
              reads=[rcb, onesfb], writes=[psb[7]])
        P.add("act", lambda e: e.activation(out=rbt[:, 0:n], in_=ps[7][:, 0:n], func=AF.Copy), reads=[psb[7]], writes=[rbb])
        oi = state["osn"] % 2
        state["osn"] += 1
        P.add("dve", lambda e: e.tensor_tensor(out=ost[oi][q0:q0 + 64, 0:n], in0=ps[acc][q0:q0 + 64, 0:n], in1=rbt[q0:q0 + 64, 0:n], op=ALU.mult),
              reads=[psb[acc], rbb], writes=[ostb[oi]])
        P.add("pool", lambda e: e.dma_start(out=ao[hl * 64:(hl + 1) * 64, t0:t0 + n], in_=ost[oi][q0:q0 + 64, 0:n]),
              reads=[ostb[oi]], writes=[aob], dma=osts[oi])

    LOOK = 2
    inflight = []

    def tick():
        for pe_ in pending:
            pe_[0] -= 1
        while pending and pending[0][0] <= 0:
            emit_fin(pending.pop(0)[1])

    for un in units:
        if un["first_of_head"]:
            while inflight:
                emit_pv(inflight.pop(0))
        emit_qk(un)
        inflight.append(un)
        if len(inflight) > LOOK:
            emit_pv(inflight.pop(0))
        tick()
    while inflight:
        emit_pv(inflight.pop(0))
    while pending:
        emit_fin(pending.pop(0)[1])
    fin = sb("fin", [128, 1], F32)
    P.add("dve", lambda e: e.memset(fin[:, :], 0.0), writes=ostb)
    P.emit(nc, es)
    es.close()
    return nc


def consts2_np():
    tri = np.zeros((128, 4, 512), np.float32)
    p = np.arange(128)[:, None]
    q = np.arange(512)[None, :]
    for o in range(4):
        tri[:, o, :] = np.where(o * 128 + p > q, NEG, 0.0)
    return np.ascontiguousarray(np.concatenate([tri.reshape(128, 2048), np.eye(128, dtype=np.float32),
                                                np.ones((128, 128), np.float32)], axis=1))


def job_of(c, g):
    return c // 4, 3 - g, (c + g) % 4


def run_l2(Qs, Ks, Vs, LF):
    nc = build_l2()
    cst = consts2_np()
    in_maps = []
    for c in range(NCORE):
        m = {"cst2": cst}
        for g in range(4):
            b, r, hg = job_of(c, g)
            hc, NK, NKB = job_dims(g)
            s = CH * r
            m["q%d" % g] = np.ascontiguousarray(Qs[b][hg * 256:(hg + 1) * 256, s:s + NQ].reshape(4, HD, NQ))
            m["k%d" % g] = np.ascontiguousarray(Ks[b][hg * 256:(hg + 1) * 256, 0:NK].reshape(4, HD, NK))
            Vh = Vs[b][0:NK, hg * 256:(hg + 1) * 256].reshape(NK, 4, HD)
            Hb = 16 * hc
            vt = np.zeros((4, NKB, 128, HD), Vh.dtype)
            vt[:, 0:Hb] = Vh[0:Hb * 128].reshape(Hb, 128, 4, HD).transpose(2, 0, 1, 3)
            vt[:, Hb, 0:16] = Vh[Hb * 128:Hb * 128 + 16].transpose(1, 0, 2)
            vt[:, Hb + 1:] = Vh[Hb * 128 + 16:].reshape(16, 128, 4, HD).transpose(2, 0, 1, 3)
            m["v%d" % g] = np.ascontiguousarray(vt.transpose(0, 2, 1, 3).reshape(4, 128, NKB * HD))
            m["l%d" % g] = np.ascontiguousarray(LF[b][0:NK, hg * 4:(hg + 1) * 4].T)
        in_maps.append(m)
    res = run_bass_kernel_spmd(nc, in_maps, core_ids=list(range(NCORE)))
    AO = [[None] * 4 for _ in range(2)]
    for b in range(2):
        for r in range(4):
            AO[b][r] = np.zeros((D, NQ), Qs[0].dtype)
    for c in range(NCORE):
        for g in range(4):
            b, r, hg = job_of(c, g)
            AO[b][r][hg * 256:(hg + 1) * 256] = np.asarray(res.results[c]["ao%d" % g])
    return AO


def build_l3(npr, cols):
    k = KB(pre=16)
    nc, P = k.nc, k.P
    NT = k.NT
    k.c_eps = cols["eps"]
    setup_consts(k, npr)
    xT = nc.dram_tensor("xT", [D, NT], F32, kind="ExternalInput").ap()
    aoT = nc.dram_tensor("aoT", [D, NT], BF16, kind="ExternalInput").ap()
    hout = nc.dram_tensor("hout", [D, CH], F32, kind="ExternalOutput").ap()
    wts = {}

    def din(name, shape):
        wts[name] = nc.dram_tensor(name, shape, F32, kind="ExternalInput").ap()
        return wts[name]

    lsem = P.dsem()
    for c in range(NKC):
        P.add("sp", lambda e, c=c: e.dma_start(out=k.h[:, c, :], in_=xT[c * 128:(c + 1) * 128, :]),
              writes=k.hb[c], dma=lsem)
    for c in range(NKC):
        for t in range(5):
            k.hb[c][t].w = (("dma", lsem), P.dma_cnt[lsem])
    asem = P.dsem()
    for c in range(NKC):
        P.add("sp", lambda e, c=c: e.dma_start(out=k.u[:, c, :], in_=aoT[c * 128:(c + 1) * 128, :]), writes=k.ub, dma=asem)
    k.next_gcol = cols["fn2b"]
    k.outproj(din("cout", [4, 128, 2048]), k.u, lambda kc, t: k.ub[t])
    k.next_gcol = cols["fn3a"]
    k.ffn(din("wg2b", [11, 128, 2048]), din("wu2b", [11, 128, 2048]), din("wd2b", [3, 4, 128, 2048]), cols["fn2b"])
    k.next_gcol = cols["mn3"]
    k.ffn(din("wg3a", [11, 128, 2048]), din("wu3a", [11, 128, 2048]), din("wd3a", [3, 4, 128, 2048]), cols["fn3a"])
    k.next_gcol = cols["fn3b"]
    k.mixer_a(din("ain1", [12, 128, 2048]), din("aout1", [4, 128, 2048]), cols["mn3"], cols["aconv1"])
    k.next_gcol = None
    k.ffn(din("wg3b", [11, 128, 2048]), din("wu3b", [11, 128, 2048]), din("wd3b", [3, 4, 128, 2048]), cols["fn3b"])
    osem = P.dsem()
    outb = Buf("out")
    for c in range(NKC):
        P.add("sp", lambda e, c=c: e.dma_start(out=hout[c * 128:(c + 1) * 128, :], in_=k.h[:, c, 16:16 + CH]),
              reads=[k.hb[c][t] for t in range(5)], writes=[outb], dma=osem)
    fin = k.es.enter_context(nc.sbuf_tensor("fin", [128, 1], F32))
    P.add("dve", lambda e: e.memset(fin[:, :], 0.0), reads=[outb])
    return k.finish()


def run_l3(inp, Hs, AO):
    pr = Params()
    cols = {}
    cols["eps"] = pr.add(np.full((128, 1), EPS, np.float32))
    cols["fn2b"] = pr.add(colvec(inp["ffn_norm"][2, 1]))
    cols["fn3a"] = pr.add(colvec(inp["ffn_norm"][3, 0]))
    cols["fn3b"] = pr.add(colvec(inp["ffn_norm"][3, 1]))
    cols["mn3"] = pr.add(colvec(inp["mix_norm"][3]))
    ac = inp["a_conv"][1]
    cols["aconv1"] = pr.add(np.concatenate([ac[:, j * 128:(j + 1) * 128].T for j in range(NKC)], axis=1))
    prm = pr.build()
    nc = build_l3(prm.shape[1], cols)
    W = {"cout": tile_in(inp["c_w_out"][0]),
         "wg2b": tile_in(inp["ffn_w_gate"][2, 1]), "wu2b": tile_in(inp["ffn_w_up"][2, 1]), "wd2b": tile_down(inp["ffn_w_down"][2, 1]),
         "wg3a": tile_in(inp["ffn_w_gate"][3, 0]), "wu3a": tile_in(inp["ffn_w_up"][3, 0]), "wd3a": tile_down(inp["ffn_w_down"][3, 0]),
         "wg3b": tile_in(inp["ffn_w_gate"][3, 1]), "wu3b": tile_in(inp["ffn_w_up"][3, 1]), "wd3b": tile_down(inp["ffn_w_down"][3, 1]),
         "ain1": tile_in(inp["a_w_in"][1]), "aout1": tile_in(inp["a_w_out"][1])}
    cst = consts_np()
    in_maps = []
    for c in range(NCORE):
        b, r = c // 4, c % 4
        s = CH * r
        m = {"xT": np.ascontiguousarray(Hs[b][s:s + NQ].T), "aoT": AO[b][r], "prm": prm, "cst": cst}
        m.update(W)
        in_maps.append(m)
    res = run_bass_kernel_spmd(nc, in_maps, core_ids=list(range(NCORE)))
    return res


def assemble(res1):
    Hs, Qs, Ks, Vs, LF = [], [], [], [], []
    for b in range(2):
        hp, qp, kp, vp, lp = [], [], [], [], []
        for r in range(4):
            o = res1.results[b * 4 + r]
            lo = 16 if r == 0 else 32
            hp.append(np.asarray(o["hout"])[:, lo:].T)
            qp.append(np.asarray(o["qT"])[:, lo:])
            kp.append(np.asarray(o["kT"])[:, lo:])
            vp.append(np.asarray(o["vo"])[lo:])
            lp.append(np.asarray(o["lfo"])[lo:])
        Hs.append(np.concatenate(hp, axis=0))
        Qs.append(np.concatenate(qp, axis=1))
        Ks.append(np.concatenate(kp, axis=1))
        Vs.append(np.concatenate(vp, axis=0))
        LF.append(np.concatenate(lp, axis=0))
    return Hs, Qs, Ks, Vs, LF


def kernel(**inputs):
    inp = prep_common(inputs)
    res1 = run_l1(inp, 8)
    Hs, Qs, Ks, Vs, LF = assemble(res1)
    AO = run_l2(Qs, Ks, Vs, LF)
    res3 = run_l3(inp, Hs, AO)
    out = np.empty((2, SEQ, D), np.float32)
    for c in range(NCORE):
        b, r = c // 4, c % 4
        out[b, r * CH:(r + 1) * CH] = np.asarray(res3.results[c]["hout"]).T
    return out
```
